# Optimizing a Trainium2 kernel written in Bass

```python
import math
import jax, jax.numpy as jnp
from jax import lax
import numpy as np

D_MODEL = 1024
BATCH = 8
SEQ = 4096
DEPTH = 1

GRID_W = 64
CTX_LEN = 256
NA_HEADS = 8
NA_HEAD_DIM = 64
NA_WIN_H = 8
NA_WIN_W = 16
RET_HEADS = 4
RET_QK_DIM = 128
RET_V_DIM = 256
RET_CHUNK = 128
D_FF = int(math.ceil(8 * D_MODEL / 3 / 128)) * 128
CONV_W = 3
ROPE_BASE = 10000.0
EPS = 1e-6

NA_W = NA_HEADS * NA_HEAD_DIM
RET_QK_W = RET_HEADS * RET_QK_DIM
RET_V_W = RET_HEADS * RET_V_DIM
IN_SPLITS = [NA_W, NA_W, NA_W, RET_QK_W, RET_QK_W, RET_V_W, RET_V_W, D_MODEL, D_MODEL]
IN_COLS = sum(IN_SPLITS)
IN_OFFSETS = [int(o) for o in np.cumsum(IN_SPLITS)[:-1]]

kernel_name = "hybrid_natten_retention_convffn_dit"


def _rmsnorm(x, g):
    xf = x.astype(jnp.float32)
    y = xf * lax.rsqrt(jnp.mean(xf * xf, axis=-1, keepdims=True) + EPS)
    return y.astype(x.dtype) * g


def _modulate(h, shift, scale):
    return h * (1.0 + scale) + shift


def _heads(t, n_heads):
    b, n, _ = t.shape
    return t.reshape(b, n, n_heads, -1).transpose(0, 2, 1, 3)


def _merge_heads(t):
    b, h, n, d = t.shape
    return t.transpose(0, 2, 1, 3).reshape(b, n, h * d)


def _rope_axis(t, pos):
    nf = t.shape[-1] // 2
    inv = ROPE_BASE ** (-jnp.arange(nf, dtype=jnp.float32) / nf)
    ang = pos.astype(jnp.float32)[:, None] * inv[None, :]
    cos = jnp.cos(ang).astype(t.dtype)
    sin = jnp.sin(ang).astype(t.dtype)
    t1, t2 = t[..., :nf], t[..., nf:]
    return jnp.concatenate([t1 * cos - t2 * sin, t1 * sin + t2 * cos], axis=-1)


def _rope2d(t, rows, cols):
    half = t.shape[-1] // 2
    return jnp.concatenate([_rope_axis(t[..., :half], rows), _rope_axis(t[..., half:], cols)], axis=-1)


def _na_latent(q, k, v, kc, vc, rpb):
    b, h, n_rows, w, hd = q.shape
    kh = min(NA_WIN_H, n_rows)
    kw = NA_WIN_W
    cols = np.arange(w)
    c_start = np.clip(cols - kw // 2, 0, w - kw)
    cidx = c_start[:, None] + np.arange(kw)[None, :]
    dc = cidx - cols[:, None] + (NA_WIN_W - 1)
    qs = q * (hd ** -0.5)

    def row_block(r):
        r_start = jnp.clip(r - kh // 2, 0, n_rows - kh)
        kb = lax.dynamic_slice_in_dim(k, r_start, kh, axis=2)
        vb = lax.dynamic_slice_in_dim(v, r_start, kh, axis=2)
        k_win = kb[:, :, :, cidx]
        v_win = vb[:, :, :, cidx]
        q_row = lax.dynamic_index_in_dim(qs, r, axis=2, keepdims=False)
        dr = r_start + jnp.arange(kh) - r + (NA_WIN_H - 1)
        bias = rpb[:, dr[None, :, None], dc[:, None, :]]
        s_loc = jnp.einsum('bhqd,bhiqjd->bhqij', q_row, k_win) + bias[None]
        s_loc = s_loc.reshape(b, h, w, kh * kw)
        s_ctx = jnp.einsum('bhqd,bhcd->bhqc', q_row, kc)
        p = jax.nn.softmax(jnp.concatenate([s_loc, s_ctx], axis=-1).astype(jnp.float32), axis=-1).astype(v.dtype)
        p_loc = p[..., :kh * kw].reshape(b, h, w, kh, kw)
        p_ctx = p[..., kh * kw:]
        return (jnp.einsum('bhqij,bhiqjd->bhqd', p_loc, v_win)
                + jnp.einsum('bhqc,bhcd->bhqd', p_ctx, vc))

    out = lax.map(row_block, jnp.arange(n_rows))
    return out.transpose(1, 2, 0, 3, 4)


def _na_context(qc, kc, vc):
    s = jnp.einsum('bhqd,bhkd->bhqk', qc * (qc.shape[-1] ** -0.5), kc)
    p = jax.nn.softmax(s.astype(jnp.float32), axis=-1).astype(vc.dtype)
    return jnp.einsum('bhqk,bhkd->bhqd', p, vc)


def _ret_scan(q, k, v, log_g, s0, strict):
    b, h, n, dk = q.shape
    dv = v.shape[-1]
    L = RET_CHUNK
    nc = n // L
    lg = log_g.astype(jnp.float32)
    pos = jnp.arange(L, dtype=jnp.float32)
    diff = pos[:, None] - pos[None, :]
    keep = (diff > 0) if strict else (diff >= 0)
    dmat = jnp.where(keep[None], jnp.exp(lg[:, None, None] * jnp.maximum(diff, 0.0)[None]), 0.0).astype(q.dtype)
    xi = jnp.exp(lg[:, None] * (pos + 1.0)[None]).astype(q.dtype)[None, :, :, None]
    zeta = jnp.exp(lg[:, None] * (L - 1.0 - pos)[None]).astype(q.dtype)[None, :, :, None]
    g_chunk = jnp.exp(lg * L).astype(q.dtype)[None, :, None, None]

    def to_chunks(t):
        return t.reshape(b, h, nc, L, t.shape[-1]).transpose(2, 0, 1, 3, 4)

    def step(state, qkv):
        qc, kc, vc = qkv
        scores = jnp.einsum('bhld,bhmd->bhlm', qc, kc) * dmat
        out = (jnp.einsum('bhlm,bhmv->bhlv', scores, vc)
               + jnp.einsum('bhld,bhdv->bhlv', qc, state) * xi)
        state = g_chunk * state + jnp.einsum('bhmd,bhmv->bhdv', kc * zeta, vc)
        return state, out

    s_fin, outs = lax.scan(step, s0, (to_chunks(q), to_chunks(k), to_chunks(v)))
    return outs.transpose(1, 2, 0, 3, 4).reshape(b, h, n, dv), s_fin


def _bi_retention(q, k, v, lg_f, lg_b, s_f, s_b):
    o_f, sf = _ret_scan(q, k, v, lg_f, s_f, False)
    o_b, sb = _ret_scan(jnp.flip(q, 2), jnp.flip(k, 2), jnp.flip(v, 2), lg_b, s_b, True)
    return o_f + jnp.flip(o_b, 2), sf, sb


def _ret_output(o, g):
    of = o.astype(jnp.float32)
    mu = jnp.mean(of, axis=-1, keepdims=True)
    var = jnp.mean(jnp.square(of - mu), axis=-1, keepdims=True)
    on = ((of - mu) * lax.rsqrt(var + EPS)).astype(o.dtype)
    return _merge_heads(on) * jax.nn.silu(g)


def _merge(o_na, o_ret, gate_na, gate_ret, w_proj_na, w_proj_ret, w_out):
    merged = jax.nn.sigmoid(gate_na) * (o_na @ w_proj_na) + jax.nn.sigmoid(gate_ret) * (o_ret @ w_proj_ret)
    return merged @ w_out


def _conv_ffn(h, w_up, conv_w, conv_b, w_down):
    u = h @ w_up
    n = u.shape[1]
    pad = CONV_W // 2
    up = jnp.pad(u, ((0, 0), (pad, pad), (0, 0)))
    u = sum(up[:, i:i + n] * conv_w[i] for i in range(CONV_W)) + conv_b
    a, val = jnp.split(u, 2, axis=-1)
    return (jax.nn.silu(a) * val) @ w_down


def setup_inputs(seed: int = 0) -> dict:
    key = jax.random.key(seed)
    ks = jax.random.split(key, 20)

    def nrm(k, shape, s):
        return jax.random.normal(k, shape, jnp.float32) * s

    base_decay = jnp.asarray(np.log(1.0 - 2.0 ** (-5.0 - np.arange(RET_HEADS))), jnp.float32)
    return {
        "x": nrm(ks[0], (BATCH, SEQ, D_MODEL), 1.0),
        "c": nrm(ks[1], (BATCH, D_MODEL), 1.0),
        "ctx": nrm(ks[2], (BATCH, CTX_LEN, D_MODEL), 1.0),
        "c_ctx": nrm(ks[3], (D_MODEL,), 1.0),
        "w_ada": nrm(ks[4], (DEPTH, D_MODEL, 6 * D_MODEL), 0.5 * D_MODEL ** -0.5),
        "b_ada": nrm(ks[5], (DEPTH, 6 * D_MODEL), 0.01),
        "norm1_g": 1.0 + nrm(ks[6], (DEPTH, D_MODEL), 0.01),
        "w_in": nrm(ks[7], (DEPTH, D_MODEL, IN_COLS), D_MODEL ** -0.5),
        "na_rpb": nrm(ks[8], (DEPTH, NA_HEADS, 2 * NA_WIN_H - 1, 2 * NA_WIN_W - 1), 0.1),
        "ret_log_decay_fwd": base_decay[None] * (1.0 + nrm(ks[9], (DEPTH, RET_HEADS), 0.05)),
        "ret_log_decay_bwd": base_decay[None] * (1.0 + nrm(ks[10], (DEPTH, RET_HEADS), 0.05)),
        "w_proj_na": nrm(ks[11], (DEPTH, NA_W, D_MODEL), NA_W ** -0.5),
        "w_proj_ret": nrm(ks[12], (DEPTH, RET_V_W, D_MODEL), RET_V_W ** -0.5),
        "w_out": nrm(ks[13], (DEPTH, D_MODEL, D_MODEL), D_MODEL ** -0.5),
        "norm2_g": 1.0 + nrm(ks[14], (DEPTH, D_MODEL), 0.01),
        "w_up": nrm(ks[15], (DEPTH, D_MODEL, 2 * D_FF), D_MODEL ** -0.5),
        "conv_w": nrm(ks[16], (DEPTH, CONV_W, 2 * D_FF), CONV_W ** -0.5),
        "conv_b": nrm(ks[17], (DEPTH, 2 * D_FF), 0.01),
        "w_down": nrm(ks[18], (DEPTH, D_FF, D_MODEL), D_FF ** -0.5),
        "final_norm_g": 1.0 + nrm(ks[19], (D_MODEL,), 0.01),
    }


def reference(x, c, ctx, c_ctx, w_ada, b_ada, norm1_g, w_in, na_rpb, ret_log_decay_fwd,
              ret_log_decay_bwd, w_proj_na, w_proj_ret, w_out, norm2_g, w_up, conv_w, conv_b,
              w_down, final_norm_g):
    b, n, _ = x.shape
    n_rows = n // GRID_W
    t = jnp.arange(n)
    rows = t // GRID_W
    cols = t % GRID_W
    c_silu = jax.nn.silu(c)[:, None, :]
    cctx_silu = jax.nn.silu(c_ctx)[None, :]

    for l in range(DEPTH):
        last = l == DEPTH - 1
        sh1, sc1, g1, sh2, sc2, g2 = jnp.split(c_silu @ w_ada[l] + b_ada[l], 6, axis=-1)
        csh1, csc1, cg1, csh2, csc2, cg2 = jnp.split(cctx_silu @ w_ada[l] + b_ada[l], 6, axis=-1)

        hx = _modulate(_rmsnorm(x, norm1_g[l]), sh1, sc1)
        hc = _modulate(_rmsnorm(ctx, norm1_g[l]), csh1, csc1)
        (xq, xk, xv, xrq, xrk, xrv, xrg, xga, xgb) = jnp.split(hx @ w_in[l], IN_OFFSETS, axis=-1)
        (cq, ck, cv, crq, crk, crv, crg, cga, cgb) = jnp.split(hc @ w_in[l], IN_OFFSETS, axis=-1)

        kc_na, vc_na = _heads(ck, NA_HEADS), _heads(cv, NA_HEADS)
        grid = lambda tt: _heads(tt, NA_HEADS).reshape(b, NA_HEADS, n_rows, GRID_W, NA_HEAD_DIM)
        o_na = _na_latent(grid(xq), grid(xk), grid(xv), kc_na, vc_na, na_rpb[l])
        o_na = _merge_heads(o_na.reshape(b, NA_HEADS, n, NA_HEAD_DIM))

        rk_scale = RET_QK_DIM ** -0.5
        qc_r, kc_r, vc_r = _heads(crq, RET_HEADS), _heads(crk, RET_HEADS) * rk_scale, _heads(crv, RET_HEADS)
        s_zero = jnp.zeros((b, RET_HEADS, RET_QK_DIM, RET_V_DIM), x.dtype)
        o_ret_c, s_f, s_b = _bi_retention(qc_r, kc_r, vc_r, ret_log_decay_fwd[l], ret_log_decay_bwd[l], s_zero, s_zero)
        q_r = _rope2d(_heads(xrq, RET_HEADS), rows, cols)
        k_r = _rope2d(_heads(xrk, RET_HEADS), rows, cols) * rk_scale
        o_ret_x, _, _ = _bi_retention(q_r, k_r, _heads(xrv, RET_HEADS), ret_log_decay_fwd[l], ret_log_decay_bwd[l], s_f, s_b)
        o_ret = _ret_output(o_ret_x, xrg)

        x = x + g1 * _merge(o_na, o_ret, xga, xgb, w_proj_na[l], w_proj_ret[l], w_out[l])
        hx2 = _modulate(_rmsnorm(x, norm2_g[l]), sh2, sc2)
        x = x + g2 * _conv_ffn(hx2, w_up[l], conv_w[l], conv_b[l], w_down[l])

        if not last:
            o_na_c = _merge_heads(_na_context(_heads(cq, NA_HEADS), kc_na, vc_na))
            o_ret_c = _ret_output(o_ret_c, crg)
            ctx = ctx + cg1 * _merge(o_na_c, o_ret_c, cga, cgb, w_proj_na[l], w_proj_ret[l], w_out[l])
            hc2 = _modulate(_rmsnorm(ctx, norm2_g[l]), csh2, csc2)
            ctx = ctx + cg2 * _conv_ffn(hc2, w_up[l], conv_w[l], conv_b[l], w_down[l])

    return _rmsnorm(x, final_norm_g)
```

```python
import math
from contextlib import ExitStack
import numpy as np
import concourse.bass as bass
import concourse.mybir as mybir
from concourse.bass_utils import run_bass_kernel_spmd

F32 = mybir.dt.float32
BF16 = mybir.dt.bfloat16
AF = mybir.ActivationFunctionType
ALU = mybir.AluOpType

D = 1024
N = 4096
CTX = 256
NT = N + CTX
GW = 64
NEG = -30000.0
KVG = 2
EPS = 1e-6
DFF = 2816
RK_SCALE = 128 ** -0.5
LN_RK = math.log(RK_SCALE)


class Prog:
    ENGS = ["sync", "scalar", "gpsimd", "vector", "tensor"]

    def __init__(self, nc):
        self.nc = nc
        self.q = {e: [] for e in self.ENGS}
        self.sems = {}
        self.cnt = {}
        self.waited = {e: {} for e in self.ENGS}
        self.ev = {e: [] for e in self.ENGS}
        self.pools = {}
        self.pool_i = {}
        for e in self.ENGS:
            self.newsem("eng_" + e)
        self.nsem = 0

    def newsem(self, name=None):
        if name is None:
            self.nsem += 1
            name = "s%d" % self.nsem
        self.sems[name] = self.nc.alloc_semaphore(name=name)
        self.cnt[name] = 0
        return name

    def _waits(self, eng, deps):
        for d in deps:
            if d is None:
                continue
            name, val = d
            if self.waited[eng].get(name, 0) >= val:
                continue
            self.waited[eng][name] = val
            sem = self.sems[name]
            self.q[eng].append(lambda e, sem=sem, val=val: e.wait_ge(sem, val))
            self.ev[eng].append(("wait", name, val))

    def op(self, eng, fn, deps=(), inc=True):
        self._waits(eng, deps)
        name = "eng_" + eng
        if inc:
            self.cnt[name] += 1
            sem = self.sems[name]
            self.q[eng].append(lambda e, fn=fn, sem=sem: fn(e).then_inc(sem, 1))
            self.ev[eng].append(("inc", name, 1))
        else:
            self.q[eng].append(lambda e, fn=fn: fn(e))
        return (name, self.cnt[name] if inc else self.cnt[name] + 1)

    def dma(self, eng, out, in_, sem=None, deps=(), **kw):
        if sem is None:
            if eng not in self.pools:
                self.pools[eng] = [self.newsem() for _ in range(8)]
            pool = self.pools[eng]
            i = self.pool_i.get(eng, 0)
            self.pool_i[eng] = i + 1
            sem = pool[i % len(pool)]
            if self.cnt[sem] > 0:
                deps = list(deps) + [(sem, self.cnt[sem])]
        self._waits(eng, deps)
        self.cnt[sem] += 16
        s = self.sems[sem]
        self.q[eng].append(lambda e, s=s: e.dma_start(out=out, in_=in_, **kw).then_inc(s, 16))
        self.ev[eng].append(("inc", sem, 16))
        return (sem, self.cnt[sem])

    def wait(self, eng, deps):
        self._waits(eng, deps)

    def all_tokens(self):
        return [(k, v) for k, v in self.cnt.items() if v > 0]

    def barrier(self):
        toks = self.all_tokens()
        for e in self.ENGS:
            self._waits(e, toks)

    def check(self):
        cnt = {k: 0 for k in self.cnt}
        pos = {e: 0 for e in self.ENGS}
        prog = True
        while prog:
            prog = False
            for e in self.ENGS:
                ev = self.ev[e]
                while pos[e] < len(ev):
                    kind, name, v = ev[pos[e]]
                    if kind == "inc":
                        cnt[name] += v
                    elif cnt[name] < v:
                        break
                    pos[e] += 1
                    prog = True
        stuck = {e: (pos[e], len(self.ev[e]), self.ev[e][pos[e]], cnt[self.ev[e][pos[e]][1]])
                 for e in self.ENGS if pos[e] < len(self.ev[e])}
        if stuck:
            raise RuntimeError("DEADLOCK in recorded program: %r" % (stuck,))

    def emit(self):
        self.check()
        with self.nc.Block() as block:
            for e in self.ENGS:
                ops = self.q[e]

                def body(engine, ops=ops):
                    for f in ops:
                        f(engine)
                getattr(block, e)(body)

    def mm(self, out, lhsT, rhs, start=True, stop=True, deps=(), inc=False):
        return self.op("tensor", lambda e: e.matmul(out, lhsT, rhs, start=start, stop=stop), deps, inc)

    def tp(self, out, in_, ident, deps=(), inc=False):
        return self.op("tensor", lambda e: e.transpose(out, in_, ident), deps, inc)

    def act(self, out, in_, func, bias=None, scale=None, accum_out=None, deps=()):
        kw = {}
        if bias is not None:
            kw["bias"] = bias
        if scale is not None:
            kw["scale"] = scale
        if accum_out is not None:
            kw["accum_out"] = accum_out
        return self.op("scalar", lambda e: e.activation(out, in_, func, **kw), deps)

    def tt(self, eng, out, in0, in1, op, deps=()):
        return self.op(eng, lambda e: e.tensor_tensor(out, in0, in1, op), deps)

    def ts(self, eng, out, in0, s1, s2, op0, op1=None, deps=()):
        if op1 is None:
            return self.op(eng, lambda e: e.tensor_scalar(out, in0, s1, None, op0), deps)
        return self.op(eng, lambda e: e.tensor_scalar(out, in0, s1, s2, op0, op1), deps)

    def stt(self, out, in0, scalar, in1, op0, op1, deps=()):
        return self.op("vector", lambda e: e.scalar_tensor_tensor(out, in0, scalar, in1, op0, op1), deps)

    def cp(self, eng, out, in_, deps=()):
        if eng == "scalar":
            return self.op(eng, lambda e: e.copy(out, in_), deps)
        return self.op(eng, lambda e: e.tensor_copy(out, in_), deps)

    def recip(self, out, in_, deps=()):
        return self.op("vector", lambda e: e.reciprocal(out, in_), deps)

    def memset(self, eng, ap, val, deps=()):
        return self.op(eng, lambda e: e.memset(ap, val), deps)


class Ring:
    def __init__(self, tiles):
        self.tiles = tiles
        self.rd = [[] for _ in tiles]
        self.i = -1

    def get(self):
        self.i = (self.i + 1) % len(self.tiles)
        deps = self.rd[self.i]
        self.rd[self.i] = []
        return self.tiles[self.i], deps, self.i

    def used(self, slot, tok):
        self.rd[slot].append(tok)


def _na_entries():
    ents = {}
    order = []

    def rs(qr):
        return min(max(qr - 4, 0), 56)

    def ent(kt, qr):
        r0 = rs(qr)
        res = []
        for kr in (2 * kt, 2 * kt + 1):
            res.append(kr - qr + 7 if r0 <= kr < r0 + 8 else None)
        return tuple(res)

    for kt in range(32):
        for qr in range(64):
            e = ent(kt, qr)
            if e not in ents:
                ents[e] = len(order)
                order.append(e)
    return ents, order, ent


NA_ENTS, NA_ORDER, NA_ENT = _na_entries()
NE = len(NA_ORDER)


def _const_layout():
    off = {}
    cur = 0

    def add(name, n):
        nonlocal cur
        off[name] = (cur, n)
        cur += n
    add("cc", 16)
    add("bada", 48)
    add("n1g", 8)
    add("n2g", 8)
    add("convw", 44 * 3)
    add("convb", 44)
    add("lgf", 4)
    add("lgb", 4)
    add("cm1", 1)
    add("cm", 1)
    add("eps", 1)
    add("ident", 128)
    return off, cur


def _const2_layout():
    off = {}
    cur = 0
    for name, n in (("dpos", 128), ("dneg", 128), ("mf", 128), ("mb", 128), ("lp1", 128), ("lml", 128),
                    ("maskneg", NE * 64)):
        off[name] = (cur, n)
        cur += n
    return off, cur


COFF, NCONST = _const_layout()
COFF2, NCONST2 = _const2_layout()


def _host_consts():
    c = np.zeros((128, NCONST), np.float32)
    c2 = np.zeros((128, NCONST2), np.float32)

    def put(name, arr):
        if name in COFF:
            o, n = COFF[name]
            c[:, o:o + n] = np.asarray(arr, np.float32).reshape(128, n)
        else:
            o, n = COFF2[name]
            c2[:, o:o + n] = np.asarray(arr, np.float32).reshape(128, n)
    m = np.arange(128)
    put("cm1", (127 - m)[:, None])
    put("cm", m[:, None])
    put("eps", np.full((128, 1), EPS))
    put("ident", np.eye(128))
    diff = m[None, :] - m[:, None]
    put("dpos", np.maximum(diff, 0))
    put("dneg", np.maximum(-diff, 0))
    put("mf", (diff >= 0))
    put("mb", (diff < 0))
    put("lp1", np.broadcast_to((m + 1)[None, :], (128, 128)))
    put("lml", np.broadcast_to((128 - m)[None, :], (128, 128)))
    mk = np.full((128, NE, 64), NEG, np.float32)
    qc = np.arange(64)
    cs = np.clip(qc - 8, 0, 48)
    kc = np.arange(64)
    colok = (kc[:, None] >= cs[None, :]) & (kc[:, None] < cs[None, :] + 16)
    for e, (u, l) in enumerate(NA_ORDER):
        if u is not None:
            mk[0:64, e, :] = np.where(colok, 0.0, NEG)
        if l is not None:
            mk[64:128, e, :] = np.where(colok, 0.0, NEG)
    put("maskneg", mk)
    inv = (10000.0 ** (-np.arange(32, dtype=np.float32) / 32)).astype(np.float32)
    p = np.arange(128)
    t = np.arange(32)
    rows = (2 * t[None, :] + p[:, None] // 64).astype(np.float32)
    colsv = np.broadcast_to((p % 64)[:, None], (128, 32)).astype(np.float32)
    ang = np.stack([rows[:, :, None] * inv[None, None, :], colsv[:, :, None] * inv[None, None, :]], axis=2)
    ang = ang.astype(np.float32)
    ropec = np.cos(ang).astype(np.float32).reshape(128, 32 * 64)
    ropes = np.sin(ang).astype(np.float32).reshape(128, 32 * 64)
    return c, c2, ropec, ropes


def _rpb_gather(rpb):
    kc = np.arange(64)
    qc = np.arange(64)
    dc = np.clip(kc[:, None] - qc[None, :] + 15, 0, 30)
    out = np.zeros((128, NE, 8, 64), np.float32)
    for e, (u, l) in enumerate(NA_ORDER):
        for half, dr in ((0, u), (1, l)):
            d = 0 if dr is None else dr
            out[half * 64:(half + 1) * 64, e, :, :] = rpb[:, d, :][:, dc].transpose(1, 0, 2)
    return out


def build(debug=False, upto=99):
    nc = bass.Bass("TRN2", target_bir_lowering=False)
    P = Prog(nc)

    def din(name, shape, dt=F32):
        return nc.dram_tensor(name, list(shape), dt, kind="ExternalInput").ap()

    def dscr(name, shape, dt):
        kind = "ExternalOutput" if debug else "Internal"
        return nc.dram_tensor(name, list(shape), dt, kind=kind).ap()

    x_d = din("x", [N, D])
    ctx_d = din("ctx", [CTX, D])
    cst_d = din("consts", [128, NCONST])
    cst2_d = din("consts2", [128, NCONST2])
    ropec_d = din("ropec", [128, 2048])
    ropes_d = din("ropes", [128, 2048])
    rpbg_d = din("rpbg", [128, NE * 8 * 64])
    badag_d = din("badag", [128, 2048])
    fng_d = din("fng", [128, 1024])
    w_ada_d = din("w_ada", [D, 6 * D])
    w_in_d = din("w_in", [D, 6656])
    w_pna_d = din("w_proj_na", [512, D])
    w_pret_d = din("w_proj_ret", [D, D])
    w_out_d = din("w_out", [D, D])
    w_up_d = din("w_up", [D, 2 * DFF])
    w_down_d = din("w_down", [DFF, D])
    y_d = nc.dram_tensor("y", [N, D], F32, kind="ExternalOutput").ap()

    hxT_d = dscr("hxT", [8, 128, NT], BF16)
    qna_d = dscr("qna", [4, 128, N], BF16)
    kna_d = dscr("kna", [4, 128, NT], BF16)
    vna_d = dscr("vna", [NT, 8 * 65], BF16)
    qr_d = dscr("qr", [4, 128, N], BF16)
    krT_d = dscr("krT", [4, 128, N], BF16)
    kr_d = dscr("kr", [128, 34, 512], BF16)
    vr_d = dscr("vr", [128, 34, 1024], BF16)
    sb_d = dscr("sb", [32, 128, 1024], BF16)
    oT_d = dscr("oT", [12, 128, N], BF16)
    x1_d = dscr("x1", [N, D], F32)
    hx2T_d = dscr("hx2T", [8, 128, N + 2], BF16)
    dbg_d = dscr("dbg", [128, 4096], F32)

    SB = lambda name, shape, dt: nc.alloc_sbuf_tensor("sb_" + name, shape, dt)
    PS = nc.alloc_psum_tensor

    cst = SB("cst", [128, NCONST], F32)
    identb = SB("identb", [128, 128], BF16)
    modv = SB("modv", [128, 6, 8], F32)
    gbc = SB("gbc", [128, 2048], F32)
    fng = SB("fng", [128, 1024], F32)
    smalls = SB("smalls", [128, 256], F32)
    mhalf = SB("mhalf", [128, 8], F32)
    tk_mh = P.memset("gpsimd", mhalf[:, :], -0.5)

    def C(name, a=None, b=None):
        o, n = COFF[name]
        if a is None:
            return cst[:, o:o + n]
        return cst[:, o + a:o + b]

    s_c = P.newsem("ld_c")
    tk_cst = P.dma("sync", cst[:, :], cst_d[:, :], None)
    tk_fng = P.dma("sync", fng[:, :], fng_d[:, :], None)
    tk_id = P.cp("vector", identb[:, :], C("ident"), deps=[tk_cst])

    psum_all = PS("psum_all", [128, 4096], F32)
    banks = [psum_all[:, i * 512:(i + 1) * 512] for i in range(8)]

    def bank_bf(i):
        return banks[i][:, :].bitcast(BF16)

    W1 = 3584
    es_w1 = ExitStack()
    w1 = es_w1.enter_context(nc.sbuf_tensor("t_w1", [128, 8, W1], BF16))
    s_w1 = P.newsem()
    tk_w1 = None
    for cb in range(7):
        tk_w1 = P.dma("gpsimd", w1[:, :, cb * 512:(cb + 1) * 512],
                      w_in_d[:, cb * 512:(cb + 1) * 512].rearrange("(k p) n -> p k n", p=128), s_w1)
    with ExitStack() as es:
        wa0 = es.enter_context(nc.sbuf_tensor("t_wa0", [128, 8, 512], F32))
        wa1 = es.enter_context(nc.sbuf_tensor("t_wa1", [128, 8, 512], F32))
        sc = es.enter_context(nc.sbuf_tensor("t_sc", [128, 8, 2], F32))
        screp = es.enter_context(nc.sbuf_tensor("t_screp", [128, 8, 128], F32))
        modcol = es.enter_context(nc.sbuf_tensor("t_modcol", [128, 48, 2], F32))
        badag = es.enter_context(nc.sbuf_tensor("t_badag", [128, 2048], F32))
        s_w = [P.newsem(), P.newsem()]
        war = Ring([wa0, wa1])
        tk_bg = P.dma("sync", badag[:, :], badag_d[:, :], None)
        ccv = C("cc").rearrange("p (k t) -> p k t", t=2)
        tk_sc = P.act(sc[:, :, :], ccv, AF.Silu, deps=[tk_cst])
        tk_rep = P.cp("vector", screp[:, :, :], sc[:, :, 0:1].broadcast_to([128, 8, 128]), deps=[tk_sc])
        pcol = banks[0][:, 0:96].rearrange("p (i t) -> p i t", t=2)
        prow_ring = Ring([banks[1], banks[2]])
        tk_col_last = None
        tk_rows = []
        for j in range(12):
            wa, wdeps, wslot = war.get()
            tk_w = P.dma("sync", wa[:, :, :], w_ada_d[:, j * 512:(j + 1) * 512].rearrange("(k p) n -> p k n", p=128),
                         s_w[wslot], deps=wdeps)
            seg = j // 2
            if seg in (0, 1, 3, 4):
                for cch in range(4):
                    idx = j * 4 + cch
                    for ko in range(8):
                        tk = P.mm(pcol[:, idx, :], wa[:, ko, cch * 128:(cch + 1) * 128], sc[:, ko, :],
                                  start=(ko == 0), stop=(ko == 7), deps=[tk_w, tk_sc], inc=(ko == 7))
                war.used(wslot, tk)
                tk_col_last = tk
            else:
                prow, pdeps, pslot = prow_ring.get()
                for ko in range(8):
                    tk = P.mm(prow[:, :], screp[:, ko, :], wa[:, ko, :], start=(ko == 0), stop=(ko == 7),
                              deps=[tk_w, tk_rep] + pdeps, inc=(ko == 7))
                war.used(wslot, tk)
                off = (0 if seg == 2 else 1024) + (j % 2) * 512
                tk2 = P.tt("vector", gbc[:, off:off + 512], prow[:, :], badag[:, off:off + 512], ALU.add,
                           deps=[tk, tk_bg])
                prow_ring.used(pslot, tk2)
                tk_rows.append(tk2)
        badac = C("bada").unsqueeze(2).broadcast_to([128, 48, 2])
        P.tt("vector", modcol[:, 0:16, :], pcol[:, 0:16, :], badac[:, 0:16, :], ALU.add, deps=[tk_col_last, tk_cst])
        tk_mc = P.tt("vector", modcol[:, 24:40, :], pcol[:, 24:40, :], badac[:, 24:40, :], ALU.add,
                     deps=[tk_col_last, tk_cst])
        t1 = P.stt(modv[:, 0, :], modcol[:, 8:16, 0], 1.0, C("n1g"), ALU.add, ALU.mult, deps=[tk_mc])
        t2 = P.cp("vector", modv[:, 1, :], modcol[:, 0:8, 0], deps=[tk_mc])
        t3 = P.stt(modv[:, 2, :], modcol[:, 8:16, 1], 1.0, C("n1g"), ALU.add, ALU.mult, deps=[tk_mc])
        t4 = P.cp("vector", modv[:, 3, :], modcol[:, 0:8, 1], deps=[tk_mc])
        t5 = P.stt(modv[:, 4, :], modcol[:, 32:40, 0], 1.0, C("n2g"), ALU.add, ALU.mult, deps=[tk_mc])
        t6 = P.cp("vector", modv[:, 5, :], modcol[:, 24:32, 0], deps=[tk_mc])
        tk_modv = t6
        if debug:
            d1 = P.cp("vector", smalls[:, 0:48], modv[:, :, :].rearrange("p a b -> p (a b)"), deps=[t6])
            s_dbg = P.newsem()
            P.dma("sync", dbg_d[:, 0:48], smalls[:, 0:48], None, deps=[d1])
            P.dma("sync", dbg_d[:, 2048:4096], gbc[:, :], None, deps=tk_rows)
        P.barrier()

    if upto < 1:
        return finish(nc, P, y_d)

    with ExitStack() as es:
        ropec = es.enter_context(nc.sbuf_tensor("t_ropec", [128, 2048], F32))
        ropes = es.enter_context(nc.sbuf_tensor("t_ropes", [128, 2048], F32))
        xt_all = es.enter_context(nc.sbuf_tensor("t_xt", [128, 8, 1024], F32))
        junk = es.enter_context(nc.sbuf_tensor("t_junk", [128, 1024], BF16))
        xn_all = es.enter_context(nc.sbuf_tensor("t_xn", [128, 4, 1024], BF16))
        hx_all = es.enter_context(nc.sbuf_tensor("t_hx", [128, 2, 8 * 512], BF16))
        stg_all = es.enter_context(nc.sbuf_tensor("t_stg", [128, 4, 512], BF16))
        vnas_all = es.enter_context(nc.sbuf_tensor("t_vnas", [128, 2, 8 * 65], BF16))
        rtmp = es.enter_context(nc.sbuf_tensor("t_rtmp", [128, 2, 4 * 512], F32))
        qkr_all = es.enter_context(nc.sbuf_tensor("t_qkr", [128, 2, 1024], BF16))
        qkT_all = es.enter_context(nc.sbuf_tensor("t_qkT", [128, 2, 1024], BF16))
        vrs_all = es.enter_context(nc.sbuf_tensor("t_vrs", [128, 2, 1024], BF16))
        tk_rc = P.dma("sync", ropec[:, :], ropec_d[:, :], None)
        tk_rs = P.dma("sync", ropes[:, :], ropes_d[:, :], None)
        xring = Ring([xt_all[:, i, :] for i in range(8)])
        s_x = [P.newsem() for _ in range(8)]
        xnring = Ring([xn_all[:, i, :] for i in range(4)])
        hxring = Ring([hx_all[:, i, :].rearrange("p (k n) -> p k n", k=8) for i in range(2)])
        tpring = Ring([0, 1])
        fmring = Ring([2, 3])
        tmring = Ring([6, 7])
        pq_ring = Ring([0])
        stgring = Ring([stg_all[:, i, :] for i in range(4)])
        vnaring = Ring([vnas_all[:, i, :].rearrange("p (h e) -> p h e", e=65) for i in range(2)])
        qkrring = Ring([qkr_all[:, i, :] for i in range(2)])
        qkTring = Ring([qkT_all[:, i, :].rearrange("p (h n) -> p h n", h=8) for i in range(2)])
        vrsring = Ring([vrs_all[:, i, :] for i in range(2)])
        s_st = P.newsem("st1")
        tk_ones = [P.memset("gpsimd", vnas_all[:, i, :].rearrange("p (h e) -> p h e", e=65)[:, :, 64:65], 1.0)
                   for i in range(2)]
        blocks = [(j, 4, False) for j in range(8)] + [(8, 2, True)]
        nb = len(blocks)
        ssA = smalls[:, 64:128]
        rtA = smalls[:, 128:192]
        rsA = smalls[:, 192:256]
        xn_of = {}
        hx_of = {}
        last_store = {}

        def tile_src(t):
            if t < 32:
                return x_d[t * 128:(t + 1) * 128, :]
            return ctx_d[(t - 32) * 128:(t - 31) * 128, :]

        x_of = {}

        def load_x(bi):
            j, nt, isctx = blocks[bi]
            for i in range(nt):
                t = j * 4 + i
                xt, xdeps, xs = xring.get()
                tk_ld = P.dma("sync", xt, tile_src(t), s_x[xs], deps=xdeps)
                x_of[t] = (xt, xs, tk_ld)

        def prep_vec(bi, only=None):
            j, nt, isctx = blocks[bi]
            for i in range(nt):
                if only is not None and i != only:
                    continue
                t = j * 4 + i
                xt, xs, tk_ld = x_of.pop(t)
                tk_ss = P.act(junk[:, :], xt, AF.Square, accum_out=ssA[:, t:t + 1], deps=[tk_ld])
                tk_rt = P.act(rtA[:, t:t + 1], ssA[:, t:t + 1], AF.Sqrt, scale=1.0 / D, bias=C("eps"),
                              deps=[tk_ss])
                tk_r = P.recip(rsA[:, t:t + 1], rtA[:, t:t + 1], deps=[tk_rt])
                xn, ndeps, ns = xnring.get()
                tk_xn = P.act(xn, xt, AF.Copy, scale=rsA[:, t:t + 1], deps=[tk_r] + ndeps)
                xring.used(xs, tk_xn)
                xn_of[t] = (xn, ns, tk_xn)

        def prep_pe(bi):
            j, nt, isctx = blocks[bi]
            hx, hdeps, hs = hxring.get()
            ai, si = (2, 3) if isctx else (0, 1)
            toks = []
            for i in range(nt):
                t = j * 4 + i
                xn, ns, tk_xn = xn_of[t]
                pb, pdeps, pslot = tpring.get()
                pt = bank_bf(pb).rearrange("p (k n) -> p k n", k=8)
                for k in range(8):
                    tk = P.tp(pt[:, k, :], xn[:, k * 128:(k + 1) * 128], identb[:, :], deps=[tk_xn, tk_id] + pdeps,
                              inc=(k == 7))
                xnring.used(ns, tk)
                for k in range(8):
                    tk2 = P.ts("vector", hx[:, k, i * 128:(i + 1) * 128], pt[:, k, :], modv[:, ai, k:k + 1],
                               modv[:, si, k:k + 1], ALU.mult, ALU.add, deps=[tk, tk_modv] + hdeps)
                tpring.used(pslot, tk2)
                toks.append(tk2)
            ntok = nt * 128
            t0 = j * 512
            tk_s = P.dma("sync", hxT_d[:, :, t0:t0 + ntok].rearrange("k p n -> p k n"), hx[:, :, 0:ntok], None,
                         deps=toks)
            hxring.used(hs, tk_s)
            hx_of[bi] = (hx, hs, toks, ntok, t0)

        def fm(bi):
            j, nt, isctx = blocks[bi]
            hx, hs, toks, ntok, t0 = hx_of[bi]
            for cidx in range(8):
                if cidx in (2, 4, 6) and bi + 1 < nb:
                    prep_vec(bi + 1, only=cidx // 2 - 1)
                if isctx and cidx < 4:
                    continue
                pb, pdeps, pslot = fmring.get()
                pf = banks[pb]
                for ko in range(8):
                    tk = P.mm(pf[:, 0:ntok], w1[:, ko, cidx * 128:(cidx + 1) * 128], hx[:, ko, 0:ntok],
                              start=(ko == 0), stop=(ko == 7), deps=toks + [tk_w1] + pdeps, inc=(ko == 7))
                hxring.used(hs, tk)
                stg, sdeps, ss_ = stgring.get()
                if cidx < 4:
                    tk2 = P.act(stg[:, 0:ntok], pf[:, 0:ntok], AF.Copy, scale=0.125, deps=[tk] + sdeps)
                    dst = qna_d[cidx, :, t0:t0 + ntok]
                else:
                    tk2 = P.cp("vector", stg[:, 0:ntok], pf[:, 0:ntok], deps=[tk] + sdeps)
                    dst = kna_d[cidx - 4, :, t0:t0 + ntok]
                fmring.used(pslot, tk2)
                tk3 = P.dma("sync", dst, stg[:, 0:ntok], None, deps=[tk2])
                stgring.used(ss_, tk3)

        def tm(bi):
            j, nt, isctx = blocks[bi]
            hx, hs, toks, ntok, t0 = hx_of[bi]
            for i in range(nt):
                t = j * 4 + i
                r0 = t * 128

                def lhs(ko):
                    return hx[:, ko, i * 128:(i + 1) * 128]
                pb, pdeps, pslot = tmring.get()
                for ko in range(8):
                    tk = P.mm(banks[pb][:, :], lhs(ko), w1[:, ko, 1024:1536], start=(ko == 0), stop=(ko == 7),
                              deps=toks + [tk_w1] + pdeps, inc=(ko == 7))
                vs, vdeps, vslot = vnaring.get()
                tk2 = P.act(vs[:, :, 0:64], banks[pb][:, :].rearrange("p (h e) -> p h e", e=64), AF.Copy,
                            deps=[tk, tk_ones[vslot]] + vdeps)
                tmring.used(pslot, tk2)
                tk3 = P.dma("sync", vna_d[r0:r0 + 128, :], vs.rearrange("p h e -> p (h e)"), None, deps=[tk2])
                vnaring.used(vslot, tk3)
                _, qdeps, qslot = pq_ring.get()
                for half in range(2):
                    for ko in range(8):
                        tk = P.mm(banks[4 + half][:, :], lhs(ko), w1[:, ko, 1536 + half * 512:2048 + half * 512],
                                  start=(ko == 0), stop=(ko == 7), deps=toks + [tk_w1] + qdeps,
                                  inc=(ko == 7 and half == 1))
                qk, kdeps, kslot = qkrring.get()
                if not isctx:
                    def v5(ap):
                        return ap.rearrange("p (h a b f) -> p h a b f", h=4, a=2, b=2)
                    cosb = ropec[:, t * 64:(t + 1) * 64].rearrange("p (a f) -> p a f", a=2).unsqueeze(1).broadcast_to([128, 4, 2, 32])
                    sinb = ropes[:, t * 64:(t + 1) * 64].rearrange("p (a f) -> p a f", a=2).unsqueeze(1).broadcast_to([128, 4, 2, 32])
                    rset, rdeps_, rslot_ = rtring.get()
                    rvh = [rset[:, ii * 512:(ii + 1) * 512].rearrange("p (h a f) -> p h a f", h=8, a=2) for ii in range(4)]
                    tks = []
                    for half in range(2):
                        src = v5(banks[4 + half][:, :])
                        dstv = v5(qk[:, half * 512:(half + 1) * 512])
                        t1_, t2_ = src[:, :, :, 0, :], src[:, :, :, 1, :]
                        A = rvh[0][:, half * 4:(half + 1) * 4]
                        B = rvh[1][:, half * 4:(half + 1) * 4]
                        Cc = rvh[2][:, half * 4:(half + 1) * 4]
                        Dd = rvh[3][:, half * 4:(half + 1) * 4]
                        ka = P.tt("vector", A, t1_, cosb, ALU.mult, deps=[tk, tk_rc, tk_rs] + rdeps_)
                        kb = P.tt("vector", B, t2_, sinb, ALU.mult, deps=[tk])
                        kc_ = P.tt("vector", Cc, t1_, sinb, ALU.mult, deps=[tk])
                        kd = P.tt("vector", Dd, t2_, cosb, ALU.mult, deps=[tk])
                        k1 = P.tt("gpsimd", dstv[:, :, :, 0, :], A, B, ALU.subtract, deps=[ka, kb] + kdeps)
                        k2 = P.tt("gpsimd", dstv[:, :, :, 1, :], Cc, Dd, ALU.add, deps=[kc_, kd])
                        tks += [k1, k2]
                        last_evac = kd
                    for tk_ in tks:
                        rtring.used(rslot_, tk_)
                    pq_ring.used(qslot, last_evac)
                    tk_qk = tks
                else:
                    tkc = P.cp("vector", qk[:, 512:1024], banks[5][:, :], deps=[tk] + kdeps)
                    pq_ring.used(qslot, tkc)
                    tk_qk = [tkc]
                tk3 = P.dma("sync", kr_d[:, t, :], qk[:, 512:1024], None, deps=tk_qk)
                qkrring.used(kslot, tk3)
                vr_, rdeps, rslot = vrsring.get()
                tke = []
                for half in range(2):
                    pb, pdeps, pslot = tmring.get()
                    for ko in range(8):
                        tk = P.mm(banks[pb][:, :], lhs(ko), w1[:, ko, 2560 + half * 512:3072 + half * 512],
                                  start=(ko == 0), stop=(ko == 7), deps=toks + [tk_w1] + pdeps, inc=(ko == 7))
                    if half == 0:
                        tk2 = P.cp("scalar", vr_[:, 0:512], banks[pb][:, :], deps=[tk] + rdeps)
                    else:
                        tk2 = P.cp("vector", vr_[:, 512:1024], banks[pb][:, :], deps=[tk] + rdeps)
                    tmring.used(pslot, tk2)
                    tke.append(tk2)
                hxring.used(hs, tk)
                tk3 = P.dma("sync", vr_d[:, t, :], vr_, None, deps=tke)
                vrsring.used(rslot, tk3)
                if not isctx:
                    pb2, pdeps2, pslot2 = tpring.get()
                    pt = bank_bf(pb2).rearrange("p (k n) -> p k n", k=8)
                    for k in range(8):
                        tk = P.tp(pt[:, k, :], qk[:, k * 128:(k + 1) * 128], identb[:, :], deps=tk_qk + pdeps2,
                                  inc=(k == 7))
                    qkrring.used(kslot, tk)
                    qT, tdeps, tslot = qkTring.get()
                    tk2 = P.cp("scalar", qT[:, :, :], pt, deps=[tk] + tdeps)
                    tpring.used(pslot2, tk2)
                    c0 = t * 128
                    tk3 = P.dma("sync", qr_d[:, :, c0:c0 + 128].rearrange("h p n -> p h n"), qT[:, 0:4, :], None,
                                deps=[tk2])
                    tk4 = P.dma("sync", krT_d[:, :, c0:c0 + 128].rearrange("h p n -> p h n"), qT[:, 4:8, :], None,
                                deps=[tk2])
                    qkTring.used(tslot, tk3)
                    qkTring.used(tslot, tk4)

        rtring = Ring([rtmp[:, 0, :], rtmp[:, 1, :]])
        load_x(0)
        load_x(1)
        prep_vec(0)
        prep_pe(0)
        for bi in range(nb):
            if bi + 2 < nb:
                load_x(bi + 2)
            fm(bi)
            if bi + 1 < nb:
                prep_vec(bi + 1, only=3)
                prep_pe(bi + 1)
            tm(bi)
        P.barrier()

    es_w1.close()
    if upto < 2:
        return finish(nc, P, y_d)
    es_wrg = ExitStack()
    wrg = es_wrg.enter_context(nc.sbuf_tensor("t_wrg", [128, 8, 1024], BF16, side="right"))
    s_wrg = P.newsem()
    tk_wrg = None
    for cb in range(2):
        tk_wrg = P.dma("gpsimd", wrg[:, :, cb * 512:(cb + 1) * 512],
                       w_in_d[:, 3584 + cb * 512:3584 + (cb + 1) * 512].rearrange("(k p) n -> p k n", p=128), s_wrg)
    with ExitStack() as es:
        kna_sb = es.enter_context(nc.sbuf_tensor("t_kna", [128, 4, NT], BF16))
        vna_sb = es.enter_context(nc.sbuf_tensor("t_vna", [128, 34, 520], BF16))
        qna_sb = es.enter_context(nc.sbuf_tensor("t_qna", [128, 4, N], BF16))
        btab = es.enter_context(nc.sbuf_tensor("t_btab", [128, NE, 8, 64], F32))
        mneg = es.enter_context(nc.sbuf_tensor("t_mneg", [128, NE, 64], F32))
        T_all = es.enter_context(nc.sbuf_tensor("t_T", [128, 2, 512], F32))
        PT_all = es.enter_context(nc.sbuf_tensor("t_PT", [128, 3, 512], BF16))
        rec_all = es.enter_context(nc.sbuf_tensor("t_rec", [128, 4, 8], F32))
        ona_all = es.enter_context(nc.sbuf_tensor("t_ona", [128, 2, 512], BF16))
        onaT_all = es.enter_context(nc.sbuf_tensor("t_onaT", [128, 2, 512], BF16))
        vsrc = vna_d[:, :].rearrange("(t p) e -> p t e", p=128)
        ksrc = kna_d[:, :, :].rearrange("c p n -> p c n")
        qsrc = qna_d[:, :, :].rearrange("c p n -> p c n")
        tk_b = P.dma("sync", btab[:, :, :, :].rearrange("p e h q -> p (e h q)"), rpbg_d[:, :])
        o2, n2 = COFF2["maskneg"]
        tk_m = P.dma("sync", mneg[:, :, :].rearrange("p e q -> p (e q)"), cst2_d[:, o2:o2 + n2])
        tk_kp, tk_vp, tk_qp = {}, {}, {}
        tk_kp[4] = P.dma("sync", kna_sb[:, :, N:NT], ksrc[:, :, N:NT])
        tk_vp[4] = P.dma("sync", vna_sb[:, 32:34, :], vsrc[:, 32:34, :])
        for pc in range(4):
            tk_kp[pc] = P.dma("sync", kna_sb[:, :, pc * 1024:(pc + 1) * 1024], ksrc[:, :, pc * 1024:(pc + 1) * 1024])
            tk_qp[pc] = P.dma("sync", qna_sb[:, :, pc * 1024:(pc + 1) * 1024], qsrc[:, :, pc * 1024:(pc + 1) * 1024])
            tk_vp[pc] = P.dma("sync", vna_sb[:, pc * 8:(pc + 1) * 8, :], vsrc[:, pc * 8:(pc + 1) * 8, :])
        tk_bt = None
        for e0 in range(0, NE, 4):
            e1 = min(NE, e0 + 4)
            tk_bt = P.tt("gpsimd", btab[:, e0:e1, :, :], btab[:, e0:e1, :, :],
                         mneg[:, e0:e1, :].unsqueeze(2).broadcast_to([128, e1 - e0, 8, 64]), ALU.add, deps=[tk_b, tk_m])
        Sring = Ring([0, 1, 2, 3])
        Oring = Ring([4, 5, 6])
        tpring = Ring([7])
        Tring = Ring([T_all[:, i, :] for i in range(2)])
        PTring = Ring([PT_all[:, i, :] for i in range(3)])
        onaring = Ring([ona_all[:, i, :] for i in range(2)])
        onaTring = Ring([onaT_all[:, i, :] for i in range(2)])
        units = []
        for qt in range(32):
            kts = [kt for kt in range(32) if any(NA_ENT(kt, 2 * qt + r) != (None, None) for r in range(2))]
            steps = [(32, False), (33, False)] + [(kt, True) for kt in kts]
            for si, (kt, loc) in enumerate(steps):
                for g in range(2):
                    units.append((qt, g, si, len(steps), kt, loc))
        state = {}

        def emit_S(u):
            qt, g, si, ns, kt, loc = u
            pb, pdeps, pslot = Sring.get()
            ps = g * 64
            for hh in range(4):
                tk = P.mm(banks[pb][:, hh * 128:(hh + 1) * 128], kna_sb[ps:ps + 64, hh, kt * 128:(kt + 1) * 128],
                          qna_sb[ps:ps + 64, hh, qt * 128:(qt + 1) * 128], start=True, stop=True,
                          deps=[tk_kp[kt // 8], tk_qp[qt // 8]] + pdeps, inc=(hh == 3))
            state[u] = (pb, pslot, tk)

        def emit_rest(u):
            qt, g, si, ns, kt, loc = u
            pb, pslot, tk_s = state.pop(u)
            Sv = banks[pb][:, :].rearrange("p (h q) -> p h q", h=4)
            pt, ptdeps, ptslot = PTring.get()
            ptv = pt.rearrange("p (h q) -> p h q", h=4)
            if loc:
                Tt, tdeps, tslot = Tring.get()
                Tv = Tt.rearrange("p (h q) -> p h q", h=4)
                for r in range(2):
                    e = NA_ENTS[NA_ENT(kt, 2 * qt + r)]
                    tk_a = P.tt("vector", Tv[:, :, r * 64:(r + 1) * 64], Sv[:, :, r * 64:(r + 1) * 64],
                                btab[:, e, g:8:2, :], ALU.add, deps=[tk_s, tk_bt] + tdeps)
                Sring.used(pslot, tk_a)
                tk_e = P.act(pt, Tt, AF.Exp, deps=[tk_a] + ptdeps)
                Tring.used(tslot, tk_e)
            else:
                tk_e = P.act(pt, banks[pb][:, :], AF.Exp, deps=[tk_s] + ptdeps)
                Sring.used(pslot, tk_e)
            if si == 0:
                ob, odeps, oslot = Oring.get()
                state[("O", qt, g)] = (ob, oslot)
            else:
                ob, oslot = state[("O", qt, g)]
                odeps = []
            for hh in range(4):
                h = 2 * hh + g
                first = (si == 0 and hh == 0)
                tk = P.op("tensor", lambda e, o=banks[ob][:, hh * 65:(hh + 1) * 65], l=ptv[:, hh, :],
                          r=vna_sb[:, kt, h * 65:(h + 1) * 65], st=first, sp=(si == ns - 1):
                          e.matmul(o, l, r, start=st, stop=sp, skip_group_check=True),
                          deps=[tk_e, tk_vp[kt // 8]] + odeps, inc=(hh == 3))
            PTring.used(ptslot, tk)
            if si == ns - 1:
                Ov = banks[ob][:, 0:260].rearrange("p (h e) -> p h e", e=65)
                if g == 0:
                    on_, ondeps, onslot = onaring.get()
                    state[("ona", qt)] = (on_, onslot)
                else:
                    on_, onslot = state[("ona", qt)]
                    ondeps = []
                rec = rec_all[:, qt % 4, g * 4:(g + 1) * 4]
                tk_r = P.recip(rec, Ov[:, :, 64], deps=[tk])
                onv = on_.rearrange("p (h e) -> p h e", e=64)
                tk_n = P.tt("vector", onv[:, g:8:2, :], Ov[:, :, 0:64], rec.unsqueeze(2).broadcast_to([128, 4, 64]),
                            ALU.mult, deps=[tk_r] + ondeps)
                Oring.used(oslot, tk_n)
                state[("n", qt, g)] = tk_n
                if g == 1:
                    pb2, pdeps2, pslot2 = tpring.get()
                    ptT = bank_bf(pb2)[:, 0:512].rearrange("p (c n) -> p c n", c=4)
                    for c in range(4):
                        tk = P.tp(ptT[:, c, :], on_[:, c * 128:(c + 1) * 128], identb[:, :],
                                  deps=[state[("n", qt, 0)], tk_n] + pdeps2, inc=(c == 3))
                    onaring.used(onslot, tk)
                    oT_, otdeps, otslot = onaTring.get()
                    tk2 = P.cp("scalar", oT_.rearrange("p (c n) -> p c n", c=4), ptT, deps=[tk] + otdeps)
                    tpring.used(pslot2, tk2)
                    tk3 = P.dma("sync", oT_d[0:4, :, qt * 128:(qt + 1) * 128].rearrange("c p n -> p c n"),
                                oT_.rearrange("p (c n) -> p c n", c=4), None, deps=[tk2])
                    onaTring.used(otslot, tk3)

        emit_S(units[0])
        emit_S(units[1])
        for i in range(0, len(units), 2):
            if i + 2 < len(units):
                emit_S(units[i + 2])
                emit_S(units[i + 3])
            emit_rest(units[i])
            emit_rest(units[i + 1])
        P.barrier()

    if upto < 3:
        return finish(nc, P, y_d)

    es_w3 = ExitStack()
    wg = es_w3.enter_context(nc.sbuf_tensor("t3_wg", [128, 8, 2048], BF16))
    wpn = es_w3.enter_context(nc.sbuf_tensor("t3_wpn", [128, 4, 1024], BF16))
    wpr = es_w3.enter_context(nc.sbuf_tensor("t3_wpr", [128, 8, 1024], BF16))
    wo = es_w3.enter_context(nc.sbuf_tensor("t3_wo", [128, 8, 1024], BF16))
    s_w3 = P.newsem()
    tk_w3 = None
    for cb in range(4):
        tk_w3 = P.dma("gpsimd", wg[:, :, cb * 512:(cb + 1) * 512],
                      w_in_d[:, 4608 + cb * 512:4608 + (cb + 1) * 512].rearrange("(k p) n -> p k n", p=128), s_w3)
    for cb in range(2):
        tk_w3 = P.dma("gpsimd", wpn[:, :, cb * 512:(cb + 1) * 512],
                      w_pna_d[:, cb * 512:(cb + 1) * 512].rearrange("(k p) n -> p k n", p=128), s_w3)
        tk_w3 = P.dma("gpsimd", wpr[:, :, cb * 512:(cb + 1) * 512],
                      w_pret_d[:, cb * 512:(cb + 1) * 512].rearrange("(k p) n -> p k n", p=128), s_w3)
        tk_w3 = P.dma("gpsimd", wo[:, :, cb * 512:(cb + 1) * 512],
                      w_out_d[:, cb * 512:(cb + 1) * 512].rearrange("(k p) n -> p k n", p=128), s_w3)
    with ExitStack() as es:
        c2 = es.enter_context(nc.sbuf_tensor("t_c2", [128, 768], F32))
        dmT = es.enter_context(nc.sbuf_tensor("t_dmT", [128, 4, 128], F32))
        dtmp = es.enter_context(nc.sbuf_tensor("t_dtmp", [128, 8, 128], F32))
        xi_all = es.enter_context(nc.sbuf_tensor("t_xi", [128, 2, 512], BF16))
        zg = es.enter_context(nc.sbuf_tensor("t_zg", [128, 4, 4], F32))
        S_all = es.enter_context(nc.sbuf_tensor("t_S", [128, 4, 1024], F32))
        Sbf_all = es.enter_context(nc.sbuf_tensor("t_Sbf", [128, 3, 1024], BF16))
        K_all = es.enter_context(nc.sbuf_tensor("t_K", [128, 2, KVG * 512], BF16))
        V_all = es.enter_context(nc.sbuf_tensor("t_V", [128, 2, KVG * 1024], BF16))
        qT_all = es.enter_context(nc.sbuf_tensor("t_qT", [128, 2, 512], BF16))
        kT_all = es.enter_context(nc.sbuf_tensor("t_kT", [128, 2, 512], BF16))
        sbi_all = es.enter_context(nc.sbuf_tensor("t_sbi", [128, 2, 1024], BF16))
        hxi_all = es.enter_context(nc.sbuf_tensor("t_hxi", [128, 2, 1024], BF16))
        Kz_all = es.enter_context(nc.sbuf_tensor("t_Kz", [128, 2, 512], BF16))
        PTr_all = es.enter_context(nc.sbuf_tensor("t_PTr", [128, 2, 512], BF16))
        qx_all = es.enter_context(nc.sbuf_tensor("t_qx", [128, 4, 512], BF16))
        on_all = es.enter_context(nc.sbuf_tensor("t_on", [128, 2, 1024], F32))
        sg_all = es.enter_context(nc.sbuf_tensor("t_sg", [128, 2, 1024], F32))
        oret_all = es.enter_context(nc.sbuf_tensor("t_oret", [128, 2, 1024], BF16))
        oretT_all = es.enter_context(nc.sbuf_tensor("t_oretT", [128, 2, 1024], BF16))
        st_all = es.enter_context(nc.sbuf_tensor("t_st", [128, 32, 40], F32))
        tk_c2 = P.dma("sync", c2[:, :], cst2_d[:, 0:768])

        def C2(name):
            o, n = COFF2[name]
            return c2[:, o:o + n]
        lgf, lgb = C("lgf"), C("lgb")
        zf, zb, gLf, gLb = zg[:, 0, :], zg[:, 1, :], zg[:, 2, :], zg[:, 3, :]
        xif = xi_all[:, 0, :].rearrange("p (h n) -> p h n", h=4)
        xib = xi_all[:, 1, :].rearrange("p (h n) -> p h n", h=4)
        tks_tab = []
        for h in range(4):
            a1_ = P.act(dtmp[:, 2 * h, :], C2("dpos"), AF.Exp, scale=lgf[:, h:h + 1], deps=[tk_c2, tk_cst])
            a2_ = P.act(dtmp[:, 2 * h + 1, :], C2("dneg"), AF.Exp, scale=lgb[:, h:h + 1], deps=[tk_c2])
            b1_ = P.tt("vector", dtmp[:, 2 * h, :], dtmp[:, 2 * h, :], C2("mf"), ALU.mult, deps=[a1_])
            b2_ = P.tt("vector", dtmp[:, 2 * h + 1, :], dtmp[:, 2 * h + 1, :], C2("mb"), ALU.mult, deps=[a2_])
            b3_ = P.tt("vector", dmT[:, h, :], dtmp[:, 2 * h, :], dtmp[:, 2 * h + 1, :], ALU.add, deps=[b1_, b2_])
            b4_ = P.ts("vector", dmT[:, h, :], dmT[:, h, :], RK_SCALE, None, ALU.mult, deps=[b3_])
            a3_ = P.act(xif[:, h, :], C2("lp1"), AF.Exp, scale=lgf[:, h:h + 1], deps=[b4_])
            a4_ = P.act(xib[:, h, :], C2("lml"), AF.Exp, scale=lgb[:, h:h + 1], deps=[b4_])
            a5_ = P.act(zf[:, h:h + 1], C("cm1"), AF.Exp, scale=lgf[:, h:h + 1], deps=[tk_cst])
            a6_ = P.act(zb[:, h:h + 1], C("cm"), AF.Exp, scale=lgb[:, h:h + 1], deps=[tk_cst])
            tks_tab += [b4_, a3_, a4_, a5_, a6_]
        a7_ = P.act(gLf, lgf, AF.Exp, scale=128.0, deps=[tk_cst])
        a8_ = P.act(gLb, lgb, AF.Exp, scale=128.0, deps=[tk_cst])
        z1_ = P.ts("vector", zg[:, 0:2, :], zg[:, 0:2, :], RK_SCALE, None, ALU.mult, deps=tks_tab + [a7_, a8_])
        tk_tab = [z1_, a7_, a8_] + tks_tab

        Kring = Ring([K_all[:, i, :] for i in range(2)])
        Vring = Ring([V_all[:, i, :] for i in range(2)])
        kv_groups = {}
        qTring = Ring([qT_all[:, i, :] for i in range(2)])
        kTring = Ring([kT_all[:, i, :] for i in range(2)])
        sbiring = Ring([sbi_all[:, i, :] for i in range(2)])
        hxiring = Ring([hxi_all[:, i, :] for i in range(2)])
        Kzring = Ring([Kz_all[:, i, :] for i in range(2)])
        PTrring = Ring([PTr_all[:, i, :] for i in range(2)])
        qxring = Ring([(qx_all[:, 2 * i, :], qx_all[:, 2 * i + 1, :]) for i in range(2)])
        onring = Ring([on_all[:, i, :] for i in range(2)])
        sgring = Ring([sg_all[:, i, :] for i in range(2)])
        oretring = Ring([oret_all[:, i, :] for i in range(2)])
        oretTring = Ring([oretT_all[:, i, :] for i in range(2)])
        Sbfring = Ring([Sbf_all[:, i, :] for i in range(3)])
        Pst_rings = {"bwd": Ring([3, 5]), "fwd": Ring([3])}
        pst_mode = ["bwd"]
        Sbuf = {"f": [S_all[:, 0, :], S_all[:, 1, :]], "b": [S_all[:, 2, :], S_all[:, 3, :]]}
        tk_S0 = [P.memset("vector", Sbuf["f"][0], 0.0), P.memset("vector", Sbuf["b"][0], 0.0)]
        Pout = psum_all[:, 1 * 512:3 * 512]
        Pg = psum_all[:, 5 * 512:7 * 512]
        Sstate = {w: {"cur": 0, "tok": list(tk_S0), "casts": [[], []]} for w in ("f", "b")}

        def load_KV(t):
            g = t // KVG
            if g not in kv_groups:
                Kg, kd, ks = Kring.get()
                Vg, vd, vs_ = Vring.get()
                for g_old in [go for go, v in kv_groups.items() if v[1] == ks]:
                    del kv_groups[g_old]
                nt_ = min(KVG, 34 - KVG * g)
                tkk = P.dma("sync", Kg[:, 0:nt_ * 512].rearrange("p (t f) -> p t f", t=nt_), kr_d[:, KVG * g:KVG * g + nt_, :],
                            None, deps=kd)
                tkv = P.dma("sync", Vg[:, 0:nt_ * 1024].rearrange("p (t f) -> p t f", t=nt_), vr_d[:, KVG * g:KVG * g + nt_, :],
                            None, deps=vd)
                kv_groups[g] = (Kg, ks, tkk, Vg, vs_, tkv)
            Kg, ks, tkk, Vg, vs_, tkv = kv_groups[g]
            o = t % KVG
            return (Kg[:, o * 512:(o + 1) * 512], ks, tkk), (Vg[:, o * 1024:(o + 1) * 1024], vs_, tkv)

        def kz_prep(which, KV):
            (Kt, ks, tkk), _ = KV
            z = zf if which == "f" else zb
            Kz, zdeps, zslot = Kzring.get()
            if pst_mode[0] == "bwd":
                tk1 = P.tt("gpsimd", Kz.rearrange("p (h d) -> p h d", h=4), Kt.rearrange("p (h d) -> p h d", h=4),
                           z.unsqueeze(2).broadcast_to([128, 4, 128]), ALU.mult, deps=[tkk] + tk_tab + zdeps)
            else:
                for h in range(4):
                    tk1 = P.act(Kz[:, h * 128:(h + 1) * 128], Kt[:, h * 128:(h + 1) * 128], AF.Copy,
                                scale=z[:, h:h + 1], deps=[tkk] + tk_tab + zdeps)
            Kring.used(ks, tk1)
            return Kz, zslot, tk1

        def state_step(which, KV, kzp=None):
            (Kt, ks, tkk), (Vt, vs_, tkv) = KV
            z = zf if which == "f" else zb
            gL = gLf if which == "f" else gLb
            stt_ = Sstate[which]
            cur = stt_["cur"]
            nxt = 1 - cur
            S, Sn = Sbuf[which][cur], Sbuf[which][nxt]
            if kzp is None:
                kzp = kz_prep(which, KV)
            Kz, zslot, tk1 = kzp
            ring = Pst_rings[pst_mode[0]]
            b0, pdeps, pslot = ring.get()
            Pst = psum_all[:, b0 * 512:(b0 + 2) * 512]
            for h in range(4):
                tk2 = P.mm(Pst[:, h * 256:(h + 1) * 256], Kz[:, h * 128:(h + 1) * 128], Vt[:, h * 256:(h + 1) * 256],
                           start=True, stop=True, deps=[tk1, tkv] + pdeps, inc=(h == 3))
            Kzring.used(zslot, tk2)
            Vring.used(vs_, tk2)
            deps_s = list(stt_["tok"]) + list(stt_["casts"][nxt])
            for h in range(4):
                tk3 = P.stt(Sn[:, h * 256:(h + 1) * 256], S[:, h * 256:(h + 1) * 256], gL[:, h:h + 1],
                            Pst[:, h * 256:(h + 1) * 256], ALU.mult, ALU.add, deps=[tk2] + deps_s + tk_tab)
            ring.used(pslot, tk3)
            stt_["tok"] = [tk3]
            stt_["cur"] = nxt
            stt_["casts"][nxt] = []

        def cast_state(which):
            stt_ = Sstate[which]
            S = Sbuf[which][stt_["cur"]]
            sbf, sdeps, sslot = Sbfring.get()
            tk = P.cp("scalar", sbf, S, deps=list(stt_["tok"]) + sdeps)
            stt_["casts"][stt_["cur"]].append(tk)
            return sbf, sslot, tk

        kv0 = load_KV(32)
        kv1 = load_KV(33)
        state_step("f", kv0)
        kv0b = load_KV(32)
        state_step("f", kv1)
        kv1b = load_KV(33)
        state_step("b", kv1b)
        state_step("b", kv0b)
        sb_toks = {}
        kvq = {}
        for i in range(31, -1, -1):
            for j in (i, i - 1, i - 2):
                if j >= 1 and j not in kvq:
                    kvq[j] = load_KV(j)
            sbf, sslot, tkc = cast_state("b")
            tks = P.dma("sync", sb_d[i, :, :], sbf, None, deps=[tkc])
            sb_toks[i] = tks
            Sbfring.used(sslot, tks)
            if i > 0:
                state_step("b", kvq.pop(i))
        pst_mode[0] = "fwd"
        g_guard0 = list(Sstate["b"]["tok"])

        def load_main(i):
            c0 = i * 128
            q_, qd, qs = qTring.get()
            k_, kd, ks = kTring.get()
            s_, sd, ss_ = sbiring.get()
            h_, hd, hs = hxiring.get()
            t1 = P.dma("sync", q_.rearrange("p (h n) -> p h n", h=4), qr_d[:, :, c0:c0 + 128].rearrange("h p n -> p h n"), None, deps=qd)
            t2 = P.dma("sync", k_.rearrange("p (h n) -> p h n", h=4), krT_d[:, :, c0:c0 + 128].rearrange("h p n -> p h n"), None, deps=kd)
            t3 = P.dma("sync", s_, sb_d[i, :, :], None, deps=sd + [sb_toks[i]])
            t4 = P.dma("sync", h_.rearrange("p (k n) -> p k n", k=8), hxT_d[:, :, c0:c0 + 128].rearrange("k p n -> p k n"), None, deps=hd)
            return (q_, qs, t1), (k_, ks, t2), (s_, ss_, t3), (h_, hs, t4), load_KV(i)

        gp_pre_of = {}

        def gp_pre(i, L):
            (q_, qs, t1), _, _, _, KV = L
            (qxf, qxb), xdeps, xslot = qxring.get()
            tk_x1 = P.tt("gpsimd", qxf, q_, xi_all[:, 0, :], ALU.mult, deps=[t1] + tk_tab + xdeps)
            tk_x2 = P.tt("gpsimd", qxb, q_, xi_all[:, 1, :], ALU.mult, deps=[t1])
            qTring.used(qs, tk_x2)
            kzp = kz_prep("f", KV) if i < 31 else None
            gp_pre_of[i] = ((qxf, qxb), xslot, tk_x1, tk_x2, kzp)

        def main(i, L, Lnext):
            (q_, qs, t1), (k_, ks, t2), (s_, ss_, t3), (h_, hs, t4), KV = L
            (Kt, kks, tkk), (Vt, vs_, tkv) = KV
            c0 = i * 128
            qv = q_.rearrange("p (h n) -> p h n", h=4)
            kv = k_.rearrange("p (h n) -> p h n", h=4)
            sfbf, sfslot, tk_sf = cast_state("f")
            (qxf, qxb), xslot, tk_x1, tk_x2, kzp = gp_pre_of.pop(i)
            if i < 31:
                state_step("f", KV, kzp)
            for h in range(4):
                tk = P.mm(banks[0][:, h * 128:(h + 1) * 128], kv[:, h, :], qv[:, h, :], start=True, stop=True,
                          deps=[t1, t2] + sc_guard, inc=(h == 3))
            kTring.used(ks, tk)
            ptr, pdeps, pslot = PTrring.get()
            tk_p = P.tt("vector", ptr, banks[0][:, :], dmT[:, :, :].rearrange("p h n -> p (h n)"), ALU.mult,
                        deps=[tk] + tk_tab + pdeps)
            sc_guard.clear()
            sc_guard.append(tk_p)
            qTring.used(qs, tk)
            hv = h_.rearrange("p (k n) -> p k n", k=8)
            for half in range(2):
                for ko in range(8):
                    tk_g = P.mm(Pg[:, half * 512:(half + 1) * 512], hv[:, ko, :], wrg[:, ko, half * 512:(half + 1) * 512],
                                start=(ko == 0), stop=(ko == 7), deps=[t4, tk_wrg] + g_guard, inc=(ko == 7 and half == 1))
            hxiring.used(hs, tk_g)
            sg, sgd, sgs = sgring.get()
            tk_sg = P.act(sg, Pg, AF.Silu, deps=[tk_g] + sgd)
            g_guard.clear()
            g_guard.append(tk_sg)
            for h in range(4):
                sl = slice(h * 256, (h + 1) * 256)
                P.mm(Pout[:, sl], ptr[:, h * 128:(h + 1) * 128], Vt[:, sl], start=True, stop=False,
                     deps=[tk_p, tkv] + out_guard)
                P.mm(Pout[:, sl], qxf[:, h * 128:(h + 1) * 128], sfbf[:, sl], start=False, stop=False, deps=[tk_x1, tk_sf])
                tk_o = P.mm(Pout[:, sl], qxb[:, h * 128:(h + 1) * 128], s_[:, sl], start=False, stop=True,
                            deps=[tk_x2, t3], inc=(h == 3))
            PTrring.used(pslot, tk_o)
            qxring.used(xslot, tk_o)
            sbiring.used(ss_, tk_o)
            Sbfring.used(sfslot, tk_o)
            st = st_all[:, i, :]
            bn = st[:, 0:24].rearrange("p (h s) -> p h s", h=4)
            mv = st[:, 24:32].rearrange("p (h s) -> p h s", h=4)
            rt = st[:, 32:36]
            rstd = st[:, 36:40]
            for h in range(4):
                tk_b1 = P.op("vector", lambda e, o=bn[:, h, :], a=Pout[:, h * 256:(h + 1) * 256]: e.bn_stats(o, a), deps=[tk_o])
                tk_b2 = P.op("vector", lambda e, o=mv[:, h, :], a=bn[:, h, :]: e.bn_aggr(o, a), deps=[tk_b1])
            tk_rt = P.ts("gpsimd", rt, mv[:, :, 1], 1.0, EPS, ALU.mult, ALU.add, deps=[tk_b2])
            tk_rs = P.tt("gpsimd", rstd, rt, mhalf[:, 0:4], ALU.pow, deps=[tk_rt, tk_mh])
            if Lnext is not None:
                gp_pre(i + 1, Lnext)
            on_, ond, ons = onring.get()
            for h in range(4):
                tk_n = P.ts("vector", on_[:, h * 256:(h + 1) * 256], Pout[:, h * 256:(h + 1) * 256], mv[:, h, 0:1],
                            rstd[:, h:h + 1], ALU.subtract, ALU.mult, deps=[tk_rs] + ond)
            out_guard.clear()
            out_guard.append(tk_n)
            orr, ord_, ors = oretring.get()
            tk_m = P.tt("gpsimd", orr, on_, sg, ALU.mult, deps=[tk_n, tk_sg] + ord_)
            onring.used(ons, tk_m)
            sgring.used(sgs, tk_m)
            Kring.used(kks, tk_o)
            Vring.used(vs_, tk_o)

            def part2():
                main_tail(i, c0, orr, ors, tk_m)
            return part2

        def main_tail(i, c0, orr, ors, tk_m):
            pb2 = 7
            ptT = bank_bf(pb2).rearrange("p (c n) -> p c n", c=8)
            for c in range(8):
                tk_t = P.tp(ptT[:, c, :], orr[:, c * 128:(c + 1) * 128], identb[:, :], deps=[tk_m] + tp_guard, inc=(c == 7))
            oretring.used(ors, tk_t)
            oT_, otd, ots = oretTring.get()
            tk_e = P.cp("scalar", oT_, bank_bf(pb2), deps=[tk_t] + otd)
            tp_guard.clear()
            tp_guard.append(tk_e)
            tk_st = P.dma("sync", oT_d[4:12, :, c0:c0 + 128].rearrange("c p n -> p c n"),
                          oT_.rearrange("p (c n) -> p c n", c=8), None, deps=[tk_e])
            oretTring.used(ots, tk_st)

        sc_guard, out_guard, g_guard, tp_guard = [], [], [], []
        g_guard.extend(g_guard0)
        kv_groups.clear()
        L = load_main(0)
        gp_pre(0, L)
        pend = None
        for i in range(32):
            Ln = load_main(i + 1) if i < 31 else None
            nxt = main(i, L, Ln)
            if pend is not None:
                pend()
            pend = nxt
            L = Ln
        pend()
        P.barrier()

    es_wrg.close()
    if upto < 4:
        return finish(nc, P, y_d)
    es_wp0 = ExitStack()
    wu_p0 = es_wp0.enter_context(nc.sbuf_tensor("t4_wup0", [128, 8, 1024], BF16, side="right"))
    s_wp0 = P.newsem()
    tk_wp0 = None
    for part in range(2):
        tk_wp0 = P.dma("gpsimd", wu_p0[:, :, part * 512:(part + 1) * 512],
                       w_up_d[:, part * DFF:part * DFF + 512].rearrange("(k p) n -> p k n", p=128), s_wp0)
    TB = 256
    with ExitStack() as es:
        def T_(name, shape, dt):
            return es.enter_context(nc.sbuf_tensor("t3_" + name, shape, dt))
        hxb_all = T_("hxb", [128, 2, 8 * TB], BF16)
        oTb_all = T_("oTb", [128, 2, 12 * TB], BF16)
        mT_all = T_("mT", [128, 2, 8 * TB], BF16)
        sg_all3 = T_("sg", [128, 4, TB], F32)
        m_all = T_("m", [128, 4, TB], F32)
        xt3_all = T_("xt", [128, 6, 1024], F32)
        tmp3_all = T_("tmp", [128, 2, 1024], F32)
        x1_all = T_("x1", [128, 2, 1024], F32)
        xn2_all = T_("xn2", [128, 4, 1024], BF16)
        h2_all = T_("h2", [128, 2, 1024], BF16)
        junk3 = T_("junk", [128, 1024], BF16)
        zer = T_("zer", [128, 8], BF16)
        tk_z = P.memset("vector", zer[:, :], 0.0)
        P.dma("sync", hx2T_d[:, :, 0:1].rearrange("k p n -> p k n"), zer[:, :].unsqueeze(2), None, deps=[tk_z], allow_slow_non_contiguous=True)
        P.dma("sync", hx2T_d[:, :, N + 1:N + 2].rearrange("k p n -> p k n"), zer[:, :].unsqueeze(2), None, deps=[tk_z], allow_slow_non_contiguous=True)
        hxbring = Ring([hxb_all[:, i, :].rearrange("p (k n) -> p k n", k=8) for i in range(2)])
        oTbring = Ring([oTb_all[:, i, :].rearrange("p (k n) -> p k n", k=12) for i in range(2)])
        mTring = Ring([mT_all[:, i, :].rearrange("p (k n) -> p k n", k=8) for i in range(2)])
        sgring3 = Ring([sg_all3[:, i, :] for i in range(4)])
        mring = Ring([m_all[:, i, :] for i in range(4)])
        xt3ring = Ring([xt3_all[:, i, :] for i in range(6)])
        tmp3ring = Ring([tmp3_all[:, i, :] for i in range(2)])
        x1ring = Ring([x1_all[:, i, :] for i in range(2)])
        xn2ring = Ring([xn2_all[:, i, :] for i in range(4)])
        h2ring = Ring([h2_all[:, i, :].rearrange("p (k n) -> p k n", k=8) for i in range(2)])
        pring = Ring([0, 1, 2, 3])
        poring = Ring([4, 5])
        tpring3 = Ring([6, 7])
        ss3 = smalls[:, 64:128]
        rt3 = smalls[:, 128:192]
        rs3 = smalls[:, 192:256]
        nblk = N // TB

        def load_blk(b):
            t0 = b * TB
            hb, hd, hs = hxbring.get()
            ob, od, os_ = oTbring.get()
            tk1 = P.dma("sync", hb, hxT_d[:, :, t0:t0 + TB].rearrange("k p n -> p k n"), None, deps=hd)
            tk2 = P.dma("sync", ob, oT_d[:, :, t0:t0 + TB].rearrange("k p n -> p k n"), None, deps=od)
            xs = []
            for i in range(TB // 128):
                xt, xd, xsl = xt3ring.get()
                tkx = P.dma("sync", xt, x_d[t0 + i * 128:t0 + (i + 1) * 128, :], None, deps=xd)
                xs.append((xt, xsl, tkx))
            return (hb, hs, tk1), (ob, os_, tk2), xs

        def proc_blk(b, L):
            (hb, hs, tk1), (ob, os_, tk2), xs = L
            t0 = b * TB
            mT, md, ms = mTring.get()
            last_mm = None
            for fc in range(8):
                fsl = slice(fc * 128, (fc + 1) * 128)
                res = {}
                for which in (1, 3, 0, 2):
                    pb, pd, psl = pring.get()
                    if which == 0:
                        nk, lw, rh = 4, (lambda k: wpn[:, k, fsl]), (lambda k: ob[:, k, :])
                    elif which == 2:
                        nk, lw, rh = 8, (lambda k: wpr[:, k, fsl]), (lambda k: ob[:, 4 + k, :])
                    elif which == 1:
                        nk, lw, rh = 8, (lambda k: wg[:, k, fc * 128:(fc + 1) * 128]), (lambda k: hb[:, k, :])
                    else:
                        nk, lw, rh = 8, (lambda k: wg[:, k, 1024 + fc * 128:1024 + (fc + 1) * 128]), (lambda k: hb[:, k, :])
                    for k in range(nk):
                        tk = P.mm(banks[pb][:, 0:TB], lw(k), rh(k), start=(k == 0), stop=(k == nk - 1),
                                  deps=[tk1, tk2, tk_w3] + pd, inc=(k == nk - 1))
                    res[which] = (pb, psl, tk)
                    last_mm = tk
                sga, sad, sas = sgring3.get()
                tk_sa = P.act(sga, banks[res[1][0]][:, 0:TB], AF.Sigmoid, deps=[res[1][2]] + sad)
                pring.used(res[1][1], tk_sa)
                sgb, sbd, sbs = sgring3.get()
                tk_sb = P.act(sgb, banks[res[3][0]][:, 0:TB], AF.Sigmoid, deps=[res[3][2]] + sbd)
                pring.used(res[3][1], tk_sb)
                m1, m1d, m1s = mring.get()
                tk_m1 = P.tt("vector", m1, banks[res[0][0]][:, 0:TB], sga, ALU.mult, deps=[res[0][2], tk_sa] + m1d)
                pring.used(res[0][1], tk_m1)
                sgring3.used(sas, tk_m1)
                m2, m2d, m2s = mring.get()
                tk_m2 = P.tt("vector", m2, banks[res[2][0]][:, 0:TB], sgb, ALU.mult, deps=[res[2][2], tk_sb] + m2d)
                pring.used(res[2][1], tk_m2)
                sgring3.used(sbs, tk_m2)
                tk_mt = P.tt("gpsimd", mT[:, fc, :], m1, m2, ALU.add, deps=[tk_m1, tk_m2] + md)
                mring.used(m1s, tk_mt)
                mring.used(m2s, tk_mt)
            hxbring.used(hs, last_mm)
            oTbring.used(os_, last_mm)

            def part2():
                pend_tiles = []
                for i in range(TB // 128):
                    xt, xsl, tkx = xs[i]
                    t = (t0 // 128) + i
                    x1, x1d, x1s = x1ring.get()
                    tmp, tmd, tms = tmp3ring.get()
                    for nh in range(2):
                        pb, pd, psl = poring.get()
                        for fc in range(8):
                            tk = P.mm(banks[pb][:, :], mT[:, fc, i * 128:(i + 1) * 128], wo[:, fc, nh * 512:(nh + 1) * 512],
                                      start=(fc == 0), stop=(fc == 7), deps=[tk_mt, tk_w3] + pd, inc=(fc == 7))
                        tk_a = P.tt("vector", tmp[:, nh * 512:(nh + 1) * 512], banks[pb][:, :], gbc[:, nh * 512:(nh + 1) * 512],
                                    ALU.mult, deps=[tk] + tmd)
                        poring.used(psl, tk_a)
                        tk_b = P.tt("gpsimd", x1[:, nh * 512:(nh + 1) * 512], tmp[:, nh * 512:(nh + 1) * 512],
                                    xt[:, nh * 512:(nh + 1) * 512], ALU.add, deps=[tk_a, tkx] + x1d)
                    mTring.used(ms, tk)
                    tmp3ring.used(tms, tk_b)
                    xt3ring.used(xsl, tk_b)
                    tk_st = P.dma("sync", x1_d[t * 128:(t + 1) * 128, :], x1, None, deps=[tk_b])
                    x1ring.used(x1s, tk_st)
                    tk_ss = P.act(junk3[:, :], x1, AF.Square, accum_out=ss3[:, t:t + 1], deps=[tk_b])
                    tk_rt = P.ts("gpsimd", rt3[:, t:t + 1], ss3[:, t:t + 1], 1.0 / D, EPS, ALU.mult, ALU.add, deps=[tk_ss])
                    tk_r = P.tt("gpsimd", rs3[:, t:t + 1], rt3[:, t:t + 1], mhalf[:, 0:1], ALU.pow, deps=[tk_rt, tk_mh])
                    xn, nd, ns_ = xn2ring.get()
                    tk_xn = P.act(xn, x1, AF.Copy, scale=rs3[:, t:t + 1], deps=[tk_r] + nd)
                    x1ring.used(x1s, tk_xn)
                    pend_tiles.append((xn, ns_, tk_xn, t))

                def part2b():
                    for (xn, ns_, tk_xn, t) in pend_tiles:
                        pb2, pd2, psl2 = tpring3.get()
                        pt = bank_bf(pb2).rearrange("p (k n) -> p k n", k=8)
                        for k in range(8):
                            tk = P.tp(pt[:, k, :], xn[:, k * 128:(k + 1) * 128], identb[:, :], deps=[tk_xn] + pd2, inc=(k == 7))
                        xn2ring.used(ns_, tk)
                        h2, h2d, h2s = h2ring.get()
                        for k in range(8):
                            tk2_ = P.ts("vector", h2[:, k, :], pt[:, k, :], modv[:, 4, k:k + 1], modv[:, 5, k:k + 1],
                                        ALU.mult, ALU.add, deps=[tk] + h2d)
                        tpring3.used(psl2, tk2_)
                        tk_s2 = P.dma("sync", hx2T_d[:, :, 1 + t * 128:1 + (t + 1) * 128].rearrange("k p n -> p k n"), h2, None,
                                      deps=[tk2_])
                        h2ring.used(h2s, tk_s2)

                return part2b
            return part2

        L = load_blk(0)
        pend = None
        pendb = None
        for b in range(nblk):
            Ln = load_blk(b + 1) if b + 1 < nblk else None
            nxt = proc_blk(b, L)
            nb = pend() if pend is not None else None
            if pendb is not None:
                pendb()
            pendb = nb
            pend = nxt
            L = Ln
        nb = pend()
        if pendb is not None:
            pendb()
        nb()
        P.barrier()

    es_w3.close()
    if upto < 5:
        return finish(nc, P, y_d)

    hT_d = nc.dram_tensor("hT_lo", [11, 128, N], BF16, kind="Internal").ap()
    FB = 256
    es_w4b = ExitStack()
    wu_n = es_w4b.enter_context(nc.sbuf_tensor("t4n_wu", [128, 8, 22 * 128], BF16))
    wd_n = es_w4b.enter_context(nc.sbuf_tensor("t4n_wd", [128, 22, 1024], BF16))
    for pas in range(2):
        with ExitStack() as es:
            def T_(name, shape, dt):
                return es.enter_context(nc.sbuf_tensor("t4%d_" % pas + name, shape, dt))
            if pas == 0:
                wu = T_("wu", [128, 8, 22 * 128], BF16)
                wd = None
            else:
                wu, wd = wu_n, wd_n
            h2b_all = T_("h2b", [128, 2, 8 * (FB + 2)], BF16)
            ct_all = T_("ct", [128, 2, 7 * FB], F32)
            hT_all = T_("hT", [128, 2, 11 * FB], BF16)
            hTlo_all = T_("hTlo", [128, 3, 11 * FB], BF16) if pas == 1 else None
            if pas == 1:
                x1b_all = T_("x1b", [128, 4, 1024], F32)
                tmp4_all = T_("tmp4", [128, 1, 1024], F32)
                x2_all = T_("x2", [128, 2, 1024], F32)
                junk4 = T_("junk4", [128, 1024], BF16)
            def load_wu(dst, pas_, sems):
                toks_ = []
                for pi, cb in enumerate(range(0, 11 * 128, 512)):
                    w_ = min(512, 11 * 128 - cb)
                    tk_ = None
                    if pas_ == 0 and pi == 0:
                        toks_.append(tk_wp0)
                        continue
                    for part in range(2):
                        c0 = part * DFF + pas_ * 11 * 128
                        tk_ = P.dma("gpsimd", dst[:, :, part * 1408 + cb:part * 1408 + cb + w_],
                                    w_up_d[:, c0 + cb:c0 + cb + w_].rearrange("(k p) n -> p k n", p=128), sems[pi])
                    toks_.append(tk_)
                return toks_
            if pas == 0:
                s_w4a = [P.newsem(), P.newsem(), P.newsem()]
                s_w4b = P.newsem()
                tk_w4p = load_wu(wu, 0, s_w4a)
                tk_w4 = tk_w4p[-1]
                tk_w4n = load_wu(wu_n, 1, [s_w4b] * 3)[-1]
                for cb in range(2):
                    tk_w4n = P.dma("gpsimd", wd_n[:, :, cb * 512:(cb + 1) * 512],
                                   w_down_d[:, cb * 512:(cb + 1) * 512].rearrange("(k p) n -> p k n", p=128), s_w4b)
            else:
                tk_w4 = tk_w4n
                tk_w4p = [tk_w4n] * 3
            h2bring = Ring([h2b_all[:, i, :].rearrange("p (k n) -> p k n", k=8) for i in range(2)])
            ctring = Ring([ct_all[:, i, :].rearrange("p (s n) -> p s n", s=7) for i in range(2)])
            hTring = Ring([hT_all[:, i, :].rearrange("p (k n) -> p k n", k=11) for i in range(2)])
            if pas == 1:
                hTloring = Ring([hTlo_all[:, i, :].rearrange("p (k n) -> p k n", k=11) for i in range(3)])
            puring = Ring([0, 1, 2, 3, 6, 7])
            poring4 = Ring([4, 5])
            if pas == 1:
                x1bring = Ring([x1b_all[:, i, :] for i in range(4)])
                tmp4ring = Ring([tmp4_all[:, i, :] for i in range(1)])
                x2ring = Ring([x2_all[:, i, :] for i in range(2)])
            cw = C("convw").rearrange("p (f t) -> p f t", t=3)
            cbias = C("convb")
            ss4 = smalls[:, 64:128]
            rt4 = smalls[:, 128:192]
            rs4 = smalls[:, 192:256]
            nblk4 = N // FB

            def load4(b):
                t0 = b * FB
                hb, hd, hs = h2bring.get()
                tk1 = P.dma("sync", hb, hx2T_d[:, :, t0:t0 + FB + 2].rearrange("k p n -> p k n"), None, deps=hd)
                extra = None
                if pas == 1:
                    hlo, htd, hts = hTloring.get()
                    tk2 = P.dma("sync", hlo, hT_d[:, :, t0:t0 + FB].rearrange("k p n -> p k n"), None, deps=htd)
                    extra = (hlo, hts, tk2)
                return (hb, hs, tk1), extra

            def proc4(b, L):
                (hb, hs, tk1), extra = L
                t0 = b * FB
                hT, hdeps, hts = hTring.get()
                hoff = 0
                if pas == 1:
                    hlo, hlos, tk_hlo = extra
                tk_last_mm = None
                tk_h = None
                for fp in range(11):
                    fglob = pas * 11 + fp
                    ct, cd, cs_ = ctring.get()
                    conv_out = []
                    for part in range(2):
                        fch = part * 22 + fglob
                        pb, pd, psl = puring.get()
                        for k in range(8):
                            if pas == 0 and fp < 4:
                                wsl = wu_p0[:, k, part * 512 + fp * 128:part * 512 + (fp + 1) * 128]
                            else:
                                wsl = wu[:, k, part * 1408 + fp * 128:part * 1408 + (fp + 1) * 128]
                            tk = P.mm(banks[pb][:, 0:FB + 2], wsl,
                                      hb[:, k, :], start=(k == 0), stop=(k == 7), deps=[tk1, tk_w4p[fp // 4]] + pd, inc=(k == 7))
                        tk_last_mm = tk
                        pu = banks[pb]
                        t0_ = ct[:, part * 3 + 0, :]
                        t1_ = ct[:, part * 3 + 1, :]
                        t2_ = ct[:, part * 3 + 2, :]
                        ka = P.act(t0_, pu[:, 1:FB + 1], AF.Identity, scale=cw[:, fch, 1:2], bias=cbias[:, fch:fch + 1],
                                   deps=[tk, tk_cst] + cd)
                        kb = P.stt(t1_, pu[:, 0:FB], cw[:, fch, 0:1], t0_, ALU.mult, ALU.add, deps=[ka])
                        kc = P.stt(t2_, pu[:, 2:FB + 2], cw[:, fch, 2:3], t1_, ALU.mult, ALU.add, deps=[kb])
                        puring.used(psl, kc)
                        conv_out.append((t2_, kc))
                    sa = ct[:, 6, :]
                    ks = P.act(sa, conv_out[0][0], AF.Silu, deps=[conv_out[0][1]])
                    tk_h = P.tt("gpsimd", hT[:, hoff + fp, :], sa, conv_out[1][0], ALU.mult,
                                deps=[ks, conv_out[1][1]] + hdeps)
                    ctring.used(cs_, tk_h)
                h2bring.used(hs, tk_last_mm)
                if pas == 0:
                    tk_s = P.dma("sync", hT_d[:, :, t0:t0 + FB].rearrange("k p n -> p k n"), hT, None, deps=[tk_h])
                    hTring.used(hts, tk_s)
                    return

                def part2():
                    xs = []
                    for i in range(FB // 128):
                        xt, xd, xsl = x1bring.get()
                        tkx = P.dma("sync", xt, x1_d[t0 + i * 128:t0 + (i + 1) * 128, :], None, deps=xd)
                        xs.append((xt, xsl, tkx))
                    for i in range(FB // 128):
                        xt, xsl, tkx = xs[i]
                        t = t0 // 128 + i
                        tmp, tmd, tms = tmp4ring.get()
                        x2, x2d, x2s = x2ring.get()
                        for nh in range(2):
                            pb, pd, psl = poring4.get()
                            for fp in range(22):
                                src = hlo[:, fp, i * 128:(i + 1) * 128] if fp < 11 else hT[:, fp - 11, i * 128:(i + 1) * 128]
                                tk = P.mm(banks[pb][:, :], src, wd[:, fp, nh * 512:(nh + 1) * 512],
                                          start=(fp == 0), stop=(fp == 21), deps=[tk_h, tk_hlo, tk_w4] + pd, inc=(fp == 21))
                            tk_a = P.tt("vector", tmp[:, nh * 512:(nh + 1) * 512], banks[pb][:, :],
                                        gbc[:, 1024 + nh * 512:1024 + (nh + 1) * 512], ALU.mult, deps=[tk] + tmd)
                            poring4.used(psl, tk_a)
                            tk_b = P.tt("gpsimd", x2[:, nh * 512:(nh + 1) * 512], tmp[:, nh * 512:(nh + 1) * 512],
                                        xt[:, nh * 512:(nh + 1) * 512], ALU.add, deps=[tk_a, tkx] + x2d)
                        hTring.used(hts, tk)
                        hTloring.used(hlos, tk)
                        tmp4ring.used(tms, tk_b)
                        x1bring.used(xsl, tk_b)
                        tk_ss = P.act(junk4[:, :], x2, AF.Square, accum_out=ss4[:, t:t + 1], deps=[tk_b])
                        tk_rt = P.ts("gpsimd", rt4[:, t:t + 1], ss4[:, t:t + 1], 1.0 / D, EPS, ALU.mult, ALU.add, deps=[tk_ss])
                        tk_r = P.tt("gpsimd", rs4[:, t:t + 1], rt4[:, t:t + 1], mhalf[:, 0:1], ALU.pow, deps=[tk_rt, tk_mh])
                        tk_y = P.stt(x2, x2, rs4[:, t:t + 1], fng[:, :], ALU.mult, ALU.mult, deps=[tk_r, tk_fng])
                        tk_o = P.dma("sync", y_d[t * 128:(t + 1) * 128, :], x2, None, deps=[tk_y])
                        x2ring.used(x2s, tk_o)
                        out_toks.append(tk_o)
                return part2

            out_toks = []
            if pas == 0:
                L = load4(0)
                for b in range(nblk4):
                    Ln = load4(b + 1) if b + 1 < nblk4 else None
                    proc4(b, L)
                    L = Ln
            else:
                L = load4(0)
                pend = None
                for b in range(nblk4):
                    Ln = load4(b + 1) if b + 1 < nblk4 else None
                    nxt = proc4(b, L)
                    if pend is not None:
                        pend()
                    pend = nxt
                    L = Ln
                pend()
            P.barrier()
        if pas == 0:
            es_wp0.close()
    es_w4b.close()
    P.emit()
    return nc


def finish(nc, P, y_d):
    s = P.newsem("fin")
    z = nc.alloc_sbuf_tensor("zfin", [128, 1024], F32)
    tk = P.memset("vector", z[:, :], 0.0)
    tk2 = P.dma("sync", y_d[0:128, :], z[:, :], s, deps=[tk])
    P.wait("sync", [tk2])
    P.emit()
    return nc


def make_in_maps(inputs):
    cst, cst2, ropec, ropes = _host_consts()
    f = lambda a: np.ascontiguousarray(np.asarray(a, np.float32))
    x = f(inputs["x"]); c = f(inputs["c"]); ctx = f(inputs["ctx"]); c_ctx = f(inputs["c_ctx"])
    b_ada = f(inputs["b_ada"])[0]

    def col(v, k):
        return v.reshape(k, 128).T

    base = cst.copy()

    def put(arr, name):
        o, n = COFF[name]
        base[:, o:o + n] = np.asarray(arr, np.float32).reshape(128, n)
    put(col(b_ada, 48), "bada")
    put(col(f(inputs["norm1_g"])[0], 8), "n1g")
    put(col(f(inputs["norm2_g"])[0], 8), "n2g")
    cw = f(inputs["conv_w"])[0]
    put(np.stack([col(cw[i], 44) for i in range(3)], axis=2), "convw")
    put(col(f(inputs["conv_b"])[0], 44), "convb")
    put(np.broadcast_to(f(inputs["ret_log_decay_fwd"])[0][None, :], (128, 4)), "lgf")
    put(np.broadcast_to(f(inputs["ret_log_decay_bwd"])[0][None, :], (128, 4)), "lgb")
    rpbg = _rpb_gather(f(inputs["na_rpb"])[0]).reshape(128, -1)
    badag = np.ascontiguousarray(np.broadcast_to(np.concatenate([b_ada[2048:3072], b_ada[5120:6144]])[None, :], (128, 2048)))
    fng = np.ascontiguousarray(np.broadcast_to(f(inputs["final_norm_g"])[None, :], (128, 1024)))
    shared = {
        "ropec": ropec, "ropes": ropes, "consts2": cst2, "rpbg": np.ascontiguousarray(rpbg), "badag": badag, "fng": fng,
        "w_ada": f(inputs["w_ada"])[0], "w_in": f(inputs["w_in"])[0], "w_proj_na": f(inputs["w_proj_na"])[0],
        "w_proj_ret": f(inputs["w_proj_ret"])[0], "w_out": f(inputs["w_out"])[0], "w_up": f(inputs["w_up"])[0],
        "w_down": f(inputs["w_down"])[0],
    }
    maps = []
    for b in range(x.shape[0]):
        cb = base.copy()
        o, n = COFF["cc"]
        cb[:, o:o + n] = np.stack([col(c[b], 8), col(c_ctx, 8)], axis=2).reshape(128, 16)
        m = dict(shared)
        m["x"] = x[b]
        m["ctx"] = ctx[b]
        m["consts"] = cb
        maps.append(m)
    return maps


def kernel(**inputs):
    maps = make_in_maps(inputs)
    nc = build()
    res = run_bass_kernel_spmd(nc, maps, core_ids=list(range(len(maps))))
    return np.stack([np.asarray(r["y"], np.float32) for r in res.results], axis=0)
```

```python
import math
from contextlib import ExitStack
import numpy as np
import concourse.bass as bass
import concourse.mybir as mybir
from concourse.bass_utils import run_bass_kernel_spmd

F32 = mybir.dt.float32
BF16 = mybir.dt.bfloat16
AF = mybir.ActivationFunctionType
ALU = mybir.AluOpType

D = 1024
N = 4096
CTX = 256
NT = N + CTX
GW = 64
NEG = -30000.0
KVG = 2
EPS = 1e-6
DFF = 2816
RK_SCALE = 128 ** -0.5
LN_RK = math.log(RK_SCALE)


class Prog:
    ENGS = ["sync", "scalar", "gpsimd", "vector", "tensor"]

    def __init__(self, nc):
        self.nc = nc
        self.q = {e: [] for e in self.ENGS}
        self.sems = {}
        self.cnt = {}
        self.waited = {e: {} for e in self.ENGS}
        self.ev = {e: [] for e in self.ENGS}
        self.pools = {}
        self.pool_i = {}
        for e in self.ENGS:
            self.newsem("eng_" + e)
        self.nsem = 0

    def newsem(self, name=None):
        if name is None:
            self.nsem += 1
            name = "s%d" % self.nsem
        self.sems[name] = self.nc.alloc_semaphore(name=name)
        self.cnt[name] = 0
        return name

    def _waits(self, eng, deps):
        for d in deps:
            if d is None:
                continue
            name, val = d
            if self.waited[eng].get(name, 0) >= val:
                continue
            self.waited[eng][name] = val
            sem = self.sems[name]
            self.q[eng].append(lambda e, sem=sem, val=val: e.wait_ge(sem, val))
            self.ev[eng].append(("wait", name, val))

    def op(self, eng, fn, deps=(), inc=True):
        self._waits(eng, deps)
        name = "eng_" + eng
        if inc:
            self.cnt[name] += 1
            sem = self.sems[name]
            self.q[eng].append(lambda e, fn=fn, sem=sem: fn(e).then_inc(sem, 1))
            self.ev[eng].append(("inc", name, 1))
        else:
            self.q[eng].append(lambda e, fn=fn: fn(e))
        return (name, self.cnt[name] if inc else self.cnt[name] + 1)

    def dma(self, eng, out, in_, sem=None, deps=(), **kw):
        if sem is None:
            if eng not in self.pools:
                self.pools[eng] = [self.newsem() for _ in range(8)]
            pool = self.pools[eng]
            i = self.pool_i.get(eng, 0)
            self.pool_i[eng] = i + 1
            sem = pool[i % len(pool)]
            if self.cnt[sem] > 0:
                deps = list(deps) + [(sem, self.cnt[sem])]
        self._waits(eng, deps)
        self.cnt[sem] += 16
        s = self.sems[sem]
        self.q[eng].append(lambda e, s=s: e.dma_start(out=out, in_=in_, **kw).then_inc(s, 16))
        self.ev[eng].append(("inc", sem, 16))
        return (sem, self.cnt[sem])

    def wait(self, eng, deps):
        self._waits(eng, deps)

    def all_tokens(self):
        return [(k, v) for k, v in self.cnt.items() if v > 0]

    def barrier(self):
        toks = self.all_tokens()
        for e in self.ENGS:
            self._waits(e, toks)

    def check(self):
        cnt = {k: 0 for k in self.cnt}
        pos = {e: 0 for e in self.ENGS}
        prog = True
        while prog:
            prog = False
            for e in self.ENGS:
                ev = self.ev[e]
                while pos[e] < len(ev):
                    kind, name, v = ev[pos[e]]
                    if kind == "inc":
                        cnt[name] += v
                    elif cnt[name] < v:
                        break
                    pos[e] += 1
                    prog = True
        stuck = {e: (pos[e], len(self.ev[e]), self.ev[e][pos[e]], cnt[self.ev[e][pos[e]][1]])
                 for e in self.ENGS if pos[e] < len(self.ev[e])}
        if stuck:
            raise RuntimeError("DEADLOCK in recorded program: %r" % (stuck,))

    def emit(self):
        self.check()
        with self.nc.Block() as block:
            for e in self.ENGS:
                ops = self.q[e]

                def body(engine, ops=ops):
                    for f in ops:
                        f(engine)
                getattr(block, e)(body)

    def mm(self, out, lhsT, rhs, start=True, stop=True, deps=(), inc=False):
        return self.op("tensor", lambda e: e.matmul(out, lhsT, rhs, start=start, stop=stop), deps, inc)

    def tp(self, out, in_, ident, deps=(), inc=False):
        return self.op("tensor", lambda e: e.transpose(out, in_, ident), deps, inc)

    def act(self, out, in_, func, bias=None, scale=None, accum_out=None, deps=()):
        kw = {}
        if bias is not None:
            kw["bias"] = bias
        if scale is not None:
            kw["scale"] = scale
        if accum_out is not None:
            kw["accum_out"] = accum_out
        return self.op("scalar", lambda e: e.activation(out, in_, func, **kw), deps)

    def tt(self, eng, out, in0, in1, op, deps=()):
        return self.op(eng, lambda e: e.tensor_tensor(out, in0, in1, op), deps)

    def ts(self, eng, out, in0, s1, s2, op0, op1=None, deps=()):
        if op1 is None:
            return self.op(eng, lambda e: e.tensor_scalar(out, in0, s1, None, op0), deps)
        return self.op(eng, lambda e: e.tensor_scalar(out, in0, s1, s2, op0, op1), deps)

    def stt(self, out, in0, scalar, in1, op0, op1, deps=()):
        return self.op("vector", lambda e: e.scalar_tensor_tensor(out, in0, scalar, in1, op0, op1), deps)

    def cp(self, eng, out, in_, deps=()):
        if eng == "scalar":
            return self.op(eng, lambda e: e.copy(out, in_), deps)
        return self.op(eng, lambda e: e.tensor_copy(out, in_), deps)

    def recip(self, out, in_, deps=()):
        return self.op("vector", lambda e: e.reciprocal(out, in_), deps)

    def memset(self, eng, ap, val, deps=()):
        return self.op(eng, lambda e: e.memset(ap, val), deps)


class Ring:
    def __init__(self, tiles):
        self.tiles = tiles
        self.rd = [[] for _ in tiles]
        self.i = -1

    def get(self):
        self.i = (self.i + 1) % len(self.tiles)
        deps = self.rd[self.i]
        self.rd[self.i] = []
        return self.tiles[self.i], deps, self.i

    def used(self, slot, tok):
        self.rd[slot].append(tok)


def _na_entries():
    ents = {}
    order = []

    def rs(qr):
        return min(max(qr - 4, 0), 56)

    def ent(kt, qr):
        r0 = rs(qr)
        res = []
        for kr in (2 * kt, 2 * kt + 1):
            res.append(kr - qr + 7 if r0 <= kr < r0 + 8 else None)
        return tuple(res)

    for kt in range(32):
        for qr in range(64):
            e = ent(kt, qr)
            if e not in ents:
                ents[e] = len(order)
                order.append(e)
    return ents, order, ent


NA_ENTS, NA_ORDER, NA_ENT = _na_entries()
NE = len(NA_ORDER)


def _const_layout():
    off = {}
    cur = 0

    def add(name, n):
        nonlocal cur
        off[name] = (cur, n)
        cur += n
    add("cc", 16)
    add("bada", 48)
    add("n1g", 8)
    add("n2g", 8)
    add("convw", 44 * 3)
    add("convb", 44)
    add("lgf", 4)
    add("lgb", 4)
    add("cm1", 1)
    add("cm", 1)
    add("eps", 1)
    add("ident", 128)
    return off, cur


def _const2_layout():
    off = {}
    cur = 0
    for name, n in (("dpos", 128), ("dneg", 128), ("mf", 128), ("mb", 128), ("lp1", 128), ("lml", 128),
                    ("maskneg", NE * 64)):
        off[name] = (cur, n)
        cur += n
    return off, cur


COFF, NCONST = _const_layout()
COFF2, NCONST2 = _const2_layout()


def _host_consts():
    c = np.zeros((128, NCONST), np.float32)
    c2 = np.zeros((128, NCONST2), np.float32)

    def put(name, arr):
        if name in COFF:
            o, n = COFF[name]
            c[:, o:o + n] = np.asarray(arr, np.float32).reshape(128, n)
        else:
            o, n = COFF2[name]
            c2[:, o:o + n] = np.asarray(arr, np.float32).reshape(128, n)
    m = np.arange(128)
    put("cm1", (127 - m)[:, None])
    put("cm", m[:, None])
    put("eps", np.full((128, 1), EPS))
    put("ident", np.eye(128))
    diff = m[None, :] - m[:, None]
    put("dpos", np.maximum(diff, 0))
    put("dneg", np.maximum(-diff, 0))
    put("mf", (diff >= 0))
    put("mb", (diff < 0))
    put("lp1", np.broadcast_to((m + 1)[None, :], (128, 128)))
    put("lml", np.broadcast_to((128 - m)[None, :], (128, 128)))
    mk = np.full((128, NE, 64), NEG, np.float32)
    qc = np.arange(64)
    cs = np.clip(qc - 8, 0, 48)
    kc = np.arange(64)
    colok = (kc[:, None] >= cs[None, :]) & (kc[:, None] < cs[None, :] + 16)
    for e, (u, l) in enumerate(NA_ORDER):
        if u is not None:
            mk[0:64, e, :] = np.where(colok, 0.0, NEG)
        if l is not None:
            mk[64:128, e, :] = np.where(colok, 0.0, NEG)
    put("maskneg", mk)
    inv = (10000.0 ** (-np.arange(32, dtype=np.float32) / 32)).astype(np.float32)
    p = np.arange(128)
    t = np.arange(32)
    rows = (2 * t[None, :] + p[:, None] // 64).astype(np.float32)
    colsv = np.broadcast_to((p % 64)[:, None], (128, 32)).astype(np.float32)
    ang = np.stack([rows[:, :, None] * inv[None, None, :], colsv[:, :, None] * inv[None, None, :]], axis=2)
    ang = ang.astype(np.float32)
    ropec = np.cos(ang).astype(np.float32).reshape(128, 32 * 64)
    ropes = np.sin(ang).astype(np.float32).reshape(128, 32 * 64)
    return c, c2, ropec, ropes


def _rpb_gather(rpb):
    kc = np.arange(64)
    qc = np.arange(64)
    dc = np.clip(kc[:, None] - qc[None, :] + 15, 0, 30)
    out = np.zeros((128, NE, 8, 64), np.float32)
    for e, (u, l) in enumerate(NA_ORDER):
        for half, dr in ((0, u), (1, l)):
            d = 0 if dr is None else dr
            out[half * 64:(half + 1) * 64, e, :, :] = rpb[:, d, :][:, dc].transpose(1, 0, 2)
    return out


def build(debug=False, upto=99):
    nc = bass.Bass("TRN2", target_bir_lowering=False)
    P = Prog(nc)

    def din(name, shape, dt=F32):
        return nc.dram_tensor(name, list(shape), dt, kind="ExternalInput").ap()

    def dscr(name, shape, dt):
        kind = "ExternalOutput" if debug else "Internal"
        return nc.dram_tensor(name, list(shape), dt, kind=kind).ap()

    x_d = din("x", [N, D])
    ctx_d = din("ctx", [CTX, D])
    cst_d = din("consts", [128, NCONST])
    cst2_d = din("consts2", [128, NCONST2])
    ropec_d = din("ropec", [128, 2048])
    ropes_d = din("ropes", [128, 2048])
    rpbg_d = din("rpbg", [128, NE * 8 * 64])
    badag_d = din("badag", [128, 2048])
    fng_d = din("fng", [128, 1024])
    w_ada_d = din("w_ada", [D, 6 * D])
    w_in_d = din("w_in", [D, 6656])
    w_pna_d = din("w_proj_na", [512, D])
    w_pret_d = din("w_proj_ret", [D, D])
    w_out_d = din("w_out", [D, D])
    w_up_d = din("w_up", [D, 2 * DFF])
    w_down_d = din("w_down", [DFF, D])
    y_d = nc.dram_tensor("y", [N, D], F32, kind="ExternalOutput").ap()

    hxT_d = dscr("hxT", [8, 128, NT], BF16)
    qna_d = dscr("qna", [4, 128, N], BF16)
    kna_d = dscr("kna", [4, 128, NT], BF16)
    vna_d = dscr("vna", [NT, 8 * 65], BF16)
    qr_d = dscr("qr", [4, 128, N], BF16)
    krT_d = dscr("krT", [4, 128, N], BF16)
    kr_d = dscr("kr", [128, 34, 512], BF16)
    vr_d = dscr("vr", [128, 34, 1024], BF16)
    sb_d = dscr("sb", [32, 128, 1024], BF16)
    oT_d = dscr("oT", [12, 128, N], BF16)
    x1_d = dscr("x1", [N, D], F32)
    hx2T_d = dscr("hx2T", [8, 128, N + 2], BF16)
    dbg_d = dscr("dbg", [128, 4096], F32)

    SB = lambda name, shape, dt: nc.alloc_sbuf_tensor("sb_" + name, shape, dt)
    PS = nc.alloc_psum_tensor

    cst = SB("cst", [128, NCONST], F32)
    identb = SB("identb", [128, 128], BF16)
    modv = SB("modv", [128, 6, 8], F32)
    gbc = SB("gbc", [128, 2048], F32)
    fng = SB("fng", [128, 1024], F32)
    smalls = SB("smalls", [128, 256], F32)
    mhalf = SB("mhalf", [128, 8], F32)
    tk_mh = P.memset("gpsimd", mhalf[:, :], -0.5)

    def C(name, a=None, b=None):
        o, n = COFF[name]
        if a is None:
            return cst[:, o:o + n]
        return cst[:, o + a:o + b]

    s_c = P.newsem("ld_c")
    tk_cst = P.dma("sync", cst[:, :], cst_d[:, :], None)
    tk_fng = P.dma("sync", fng[:, :], fng_d[:, :], None)
    tk_id = P.cp("vector", identb[:, :], C("ident"), deps=[tk_cst])

    psum_all = PS("psum_all", [128, 4096], F32)
    banks = [psum_all[:, i * 512:(i + 1) * 512] for i in range(8)]

    def bank_bf(i):
        return banks[i][:, :].bitcast(BF16)

    W1 = 3584
    es_w1 = ExitStack()
    w1 = es_w1.enter_context(nc.sbuf_tensor("t_w1", [128, 8, W1], BF16))
    s_w1 = P.newsem()
    tk_w1 = None
    for cb in range(7):
        tk_w1 = P.dma("gpsimd", w1[:, :, cb * 512:(cb + 1) * 512],
                      w_in_d[:, cb * 512:(cb + 1) * 512].rearrange("(k p) n -> p k n", p=128), s_w1)
    with ExitStack() as es:
        wa0 = es.enter_context(nc.sbuf_tensor("t_wa0", [128, 8, 512], F32))
        wa1 = es.enter_context(nc.sbuf_tensor("t_wa1", [128, 8, 512], F32))
        wa2 = es.enter_context(nc.sbuf_tensor("t_wa2", [128, 8, 512], F32))
        wa3 = es.enter_context(nc.sbuf_tensor("t_wa3", [128, 8, 512], F32))
        sc = es.enter_context(nc.sbuf_tensor("t_sc", [128, 8, 2], F32))
        screp = es.enter_context(nc.sbuf_tensor("t_screp", [128, 8, 128], F32))
        modcol = es.enter_context(nc.sbuf_tensor("t_modcol", [128, 48, 2], F32))
        badag = es.enter_context(nc.sbuf_tensor("t_badag", [128, 2048], F32))
        s_w = [P.newsem() for _ in range(4)]
        war = Ring([wa0, wa1, wa2, wa3])
        tk_bg = P.dma("sync", badag[:, :], badag_d[:, :], None)
        ccv = C("cc").rearrange("p (k t) -> p k t", t=2)
        tk_sc = P.act(sc[:, :, :], ccv, AF.Silu, deps=[tk_cst])
        tk_rep = P.cp("vector", screp[:, :, :], sc[:, :, 0:1].broadcast_to([128, 8, 128]), deps=[tk_sc])
        pcol = banks[0][:, 0:96].rearrange("p (i t) -> p i t", t=2)
        prow_ring = Ring([banks[1], banks[2]])
        tk_col_last = None
        tk_rows = []
        for j in range(12):
            wa, wdeps, wslot = war.get()
            tk_w = P.dma("sync", wa[:, :, :], w_ada_d[:, j * 512:(j + 1) * 512].rearrange("(k p) n -> p k n", p=128),
                         s_w[wslot], deps=wdeps)
            seg = j // 2
            if seg in (0, 1, 3, 4):
                for cch in range(4):
                    idx = j * 4 + cch
                    for ko in range(8):
                        tk = P.mm(pcol[:, idx, :], wa[:, ko, cch * 128:(cch + 1) * 128], sc[:, ko, :],
                                  start=(ko == 0), stop=(ko == 7), deps=[tk_w, tk_sc], inc=(ko == 7))
                war.used(wslot, tk)
                tk_col_last = tk
            else:
                prow, pdeps, pslot = prow_ring.get()
                for ko in range(8):
                    tk = P.mm(prow[:, :], screp[:, ko, :], wa[:, ko, :], start=(ko == 0), stop=(ko == 7),
                              deps=[tk_w, tk_rep] + pdeps, inc=(ko == 7))
                war.used(wslot, tk)
                off = (0 if seg == 2 else 1024) + (j % 2) * 512
                tk2 = P.tt("vector", gbc[:, off:off + 512], prow[:, :], badag[:, off:off + 512], ALU.add,
                           deps=[tk, tk_bg])
                prow_ring.used(pslot, tk2)
                tk_rows.append(tk2)
        badac = C("bada").unsqueeze(2).broadcast_to([128, 48, 2])
        P.tt("vector", modcol[:, 0:16, :], pcol[:, 0:16, :], badac[:, 0:16, :], ALU.add, deps=[tk_col_last, tk_cst])
        tk_mc = P.tt("vector", modcol[:, 24:40, :], pcol[:, 24:40, :], badac[:, 24:40, :], ALU.add,
                     deps=[tk_col_last, tk_cst])
        t1 = P.stt(modv[:, 0, :], modcol[:, 8:16, 0], 1.0, C("n1g"), ALU.add, ALU.mult, deps=[tk_mc])
        t2 = P.cp("vector", modv[:, 1, :], modcol[:, 0:8, 0], deps=[tk_mc])
        t3 = P.stt(modv[:, 2, :], modcol[:, 8:16, 1], 1.0, C("n1g"), ALU.add, ALU.mult, deps=[tk_mc])
        t4 = P.cp("vector", modv[:, 3, :], modcol[:, 0:8, 1], deps=[tk_mc])
        t5 = P.stt(modv[:, 4, :], modcol[:, 32:40, 0], 1.0, C("n2g"), ALU.add, ALU.mult, deps=[tk_mc])
        t6 = P.cp("vector", modv[:, 5, :], modcol[:, 24:32, 0], deps=[tk_mc])
        tk_modv = t6
        if debug:
            d1 = P.cp("vector", smalls[:, 0:48], modv[:, :, :].rearrange("p a b -> p (a b)"), deps=[t6])
            s_dbg = P.newsem()
            P.dma("sync", dbg_d[:, 0:48], smalls[:, 0:48], None, deps=[d1])
            P.dma("sync", dbg_d[:, 2048:4096], gbc[:, :], None, deps=tk_rows)
        P.barrier()

    if upto < 1:
        return finish(nc, P, y_d)

    with ExitStack() as es:
        ropec = es.enter_context(nc.sbuf_tensor("t_ropec", [128, 2048], F32))
        ropes = es.enter_context(nc.sbuf_tensor("t_ropes", [128, 2048], F32))
        xt_all = es.enter_context(nc.sbuf_tensor("t_xt", [128, 8, 1024], F32))
        junk = es.enter_context(nc.sbuf_tensor("t_junk", [128, 1024], BF16))
        xn_all = es.enter_context(nc.sbuf_tensor("t_xn", [128, 4, 1024], BF16))
        hx_all = es.enter_context(nc.sbuf_tensor("t_hx", [128, 2, 8 * 512], BF16))
        stg_all = es.enter_context(nc.sbuf_tensor("t_stg", [128, 4, 512], BF16))
        vnas_all = es.enter_context(nc.sbuf_tensor("t_vnas", [128, 2, 8 * 65], BF16))
        rtmp = es.enter_context(nc.sbuf_tensor("t_rtmp", [128, 2, 4 * 512], F32))
        qkr_all = es.enter_context(nc.sbuf_tensor("t_qkr", [128, 2, 1024], BF16))
        qkT_all = es.enter_context(nc.sbuf_tensor("t_qkT", [128, 2, 1024], BF16))
        vrs_all = es.enter_context(nc.sbuf_tensor("t_vrs", [128, 2, 1024], BF16))
        tk_rc = P.dma("sync", ropec[:, :], ropec_d[:, :], None)
        tk_rs = P.dma("sync", ropes[:, :], ropes_d[:, :], None)
        xring = Ring([xt_all[:, i, :] for i in range(8)])
        s_x = [P.newsem() for _ in range(8)]
        xnring = Ring([xn_all[:, i, :] for i in range(4)])
        hxring = Ring([hx_all[:, i, :].rearrange("p (k n) -> p k n", k=8) for i in range(2)])
        tpring = Ring([0, 1])
        fmring = Ring([2, 3])
        tmring = Ring([6, 7])
        pq_ring = Ring([0])
        stgring = Ring([stg_all[:, i, :] for i in range(4)])
        vnaring = Ring([vnas_all[:, i, :].rearrange("p (h e) -> p h e", e=65) for i in range(2)])
        qkrring = Ring([qkr_all[:, i, :] for i in range(2)])
        qkTring = Ring([qkT_all[:, i, :].rearrange("p (h n) -> p h n", h=8) for i in range(2)])
        vrsring = Ring([vrs_all[:, i, :] for i in range(2)])
        s_st = P.newsem("st1")
        tk_ones = [P.memset("gpsimd", vnas_all[:, i, :].rearrange("p (h e) -> p h e", e=65)[:, :, 64:65], 1.0)
                   for i in range(2)]
        blocks = [(j, 4, False) for j in range(8)] + [(8, 2, True)]
        nb = len(blocks)
        ssA = smalls[:, 64:128]
        rtA = smalls[:, 128:192]
        rsA = smalls[:, 192:256]
        xn_of = {}
        hx_of = {}
        last_store = {}

        def tile_src(t):
            if t < 32:
                return x_d[t * 128:(t + 1) * 128, :]
            return ctx_d[(t - 32) * 128:(t - 31) * 128, :]

        x_of = {}

        def load_x(bi):
            j, nt, isctx = blocks[bi]
            for i in range(nt):
                t = j * 4 + i
                xt, xdeps, xs = xring.get()
                tk_ld = P.dma("sync", xt, tile_src(t), s_x[xs], deps=xdeps)
                x_of[t] = (xt, xs, tk_ld)

        def prep_vec(bi, only=None):
            j, nt, isctx = blocks[bi]
            for i in range(nt):
                if only is not None and i != only:
                    continue
                t = j * 4 + i
                xt, xs, tk_ld = x_of.pop(t)
                tk_ss = P.act(junk[:, :], xt, AF.Square, accum_out=ssA[:, t:t + 1], deps=[tk_ld])
                tk_rt = P.act(rtA[:, t:t + 1], ssA[:, t:t + 1], AF.Sqrt, scale=1.0 / D, bias=C("eps"),
                              deps=[tk_ss])
                tk_r = P.recip(rsA[:, t:t + 1], rtA[:, t:t + 1], deps=[tk_rt])
                xn, ndeps, ns = xnring.get()
                tk_xn = P.act(xn, xt, AF.Copy, scale=rsA[:, t:t + 1], deps=[tk_r] + ndeps)
                xring.used(xs, tk_xn)
                xn_of[t] = (xn, ns, tk_xn)

        def prep_pe(bi):
            j, nt, isctx = blocks[bi]
            hx, hdeps, hs = hxring.get()
            ai, si = (2, 3) if isctx else (0, 1)
            toks = []
            for i in range(nt):
                t = j * 4 + i
                xn, ns, tk_xn = xn_of[t]
                pb, pdeps, pslot = tpring.get()
                pt = bank_bf(pb).rearrange("p (k n) -> p k n", k=8)
                for k in range(8):
                    tk = P.tp(pt[:, k, :], xn[:, k * 128:(k + 1) * 128], identb[:, :], deps=[tk_xn, tk_id] + pdeps,
                              inc=(k == 7))
                xnring.used(ns, tk)
                for k in range(8):
                    tk2 = P.ts("vector", hx[:, k, i * 128:(i + 1) * 128], pt[:, k, :], modv[:, ai, k:k + 1],
                               modv[:, si, k:k + 1], ALU.mult, ALU.add, deps=[tk, tk_modv] + hdeps)
                tpring.used(pslot, tk2)
                toks.append(tk2)
            ntok = nt * 128
            t0 = j * 512
            tk_s = P.dma("sync", hxT_d[:, :, t0:t0 + ntok].rearrange("k p n -> p k n"), hx[:, :, 0:ntok], None,
                         deps=toks)
            hxring.used(hs, tk_s)
            hx_of[bi] = (hx, hs, toks, ntok, t0)

        def fm(bi):
            j, nt, isctx = blocks[bi]
            hx, hs, toks, ntok, t0 = hx_of[bi]
            for cidx in range(8):
                if cidx in (2, 4, 6) and bi + 1 < nb:
                    prep_vec(bi + 1, only=cidx // 2 - 1)
                if isctx and cidx < 4:
                    continue
                pb, pdeps, pslot = fmring.get()
                pf = banks[pb]
                for ko in range(8):
                    tk = P.mm(pf[:, 0:ntok], w1[:, ko, cidx * 128:(cidx + 1) * 128], hx[:, ko, 0:ntok],
                              start=(ko == 0), stop=(ko == 7), deps=toks + [tk_w1] + pdeps, inc=(ko == 7))
                hxring.used(hs, tk)
                stg, sdeps, ss_ = stgring.get()
                if cidx < 4:
                    tk2 = P.act(stg[:, 0:ntok], pf[:, 0:ntok], AF.Copy, scale=0.125, deps=[tk] + sdeps)
                    dst = qna_d[cidx, :, t0:t0 + ntok]
                else:
                    tk2 = P.cp("vector", stg[:, 0:ntok], pf[:, 0:ntok], deps=[tk] + sdeps)
                    dst = kna_d[cidx - 4, :, t0:t0 + ntok]
                fmring.used(pslot, tk2)
                tk3 = P.dma("sync", dst, stg[:, 0:ntok], None, deps=[tk2])
                stgring.used(ss_, tk3)

        def tm(bi):
            j, nt, isctx = blocks[bi]
            hx, hs, toks, ntok, t0 = hx_of[bi]
            for i in range(nt):
                t = j * 4 + i
                r0 = t * 128

                def lhs(ko):
                    return hx[:, ko, i * 128:(i + 1) * 128]
                pb, pdeps, pslot = tmring.get()
                for ko in range(8):
                    tk = P.mm(banks[pb][:, :], lhs(ko), w1[:, ko, 1024:1536], start=(ko == 0), stop=(ko == 7),
                              deps=toks + [tk_w1] + pdeps, inc=(ko == 7))
                vs, vdeps, vslot = vnaring.get()
                tk2 = P.act(vs[:, :, 0:64], banks[pb][:, :].rearrange("p (h e) -> p h e", e=64), AF.Copy,
                            deps=[tk, tk_ones[vslot]] + vdeps)
                tmring.used(pslot, tk2)
                tk3 = P.dma("sync", vna_d[r0:r0 + 128, :], vs.rearrange("p h e -> p (h e)"), None, deps=[tk2])
                vnaring.used(vslot, tk3)
                _, qdeps, qslot = pq_ring.get()
                for half in range(2):
                    for ko in range(8):
                        tk = P.mm(banks[4 + half][:, :], lhs(ko), w1[:, ko, 1536 + half * 512:2048 + half * 512],
                                  start=(ko == 0), stop=(ko == 7), deps=toks + [tk_w1] + qdeps,
                                  inc=(ko == 7 and half == 1))
                qk, kdeps, kslot = qkrring.get()
                if not isctx:
                    def v5(ap):
                        return ap.rearrange("p (h a b f) -> p h a b f", h=4, a=2, b=2)
                    cosb = ropec[:, t * 64:(t + 1) * 64].rearrange("p (a f) -> p a f", a=2).unsqueeze(1).broadcast_to([128, 4, 2, 32])
                    sinb = ropes[:, t * 64:(t + 1) * 64].rearrange("p (a f) -> p a f", a=2).unsqueeze(1).broadcast_to([128, 4, 2, 32])
                    rset, rdeps_, rslot_ = rtring.get()
                    rvh = [rset[:, ii * 512:(ii + 1) * 512].rearrange("p (h a f) -> p h a f", h=8, a=2) for ii in range(4)]
                    tks = []
                    for half in range(2):
                        src = v5(banks[4 + half][:, :])
                        dstv = v5(qk[:, half * 512:(half + 1) * 512])
                        t1_, t2_ = src[:, :, :, 0, :], src[:, :, :, 1, :]
                        A = rvh[0][:, half * 4:(half + 1) * 4]
                        B = rvh[1][:, half * 4:(half + 1) * 4]
                        Cc = rvh[2][:, half * 4:(half + 1) * 4]
                        Dd = rvh[3][:, half * 4:(half + 1) * 4]
                        ka = P.tt("vector", A, t1_, cosb, ALU.mult, deps=[tk, tk_rc, tk_rs] + rdeps_)
                        kb = P.tt("vector", B, t2_, sinb, ALU.mult, deps=[tk])
                        kc_ = P.tt("vector", Cc, t1_, sinb, ALU.mult, deps=[tk])
                        kd = P.tt("vector", Dd, t2_, cosb, ALU.mult, deps=[tk])
                        k1 = P.tt("gpsimd", dstv[:, :, :, 0, :], A, B, ALU.subtract, deps=[ka, kb] + kdeps)
                        k2 = P.tt("gpsimd", dstv[:, :, :, 1, :], Cc, Dd, ALU.add, deps=[kc_, kd])
                        tks += [k1, k2]
                        last_evac = kd
                    for tk_ in tks:
                        rtring.used(rslot_, tk_)
                    pq_ring.used(qslot, last_evac)
                    tk_qk = tks
                else:
                    tkc = P.cp("vector", qk[:, 512:1024], banks[5][:, :], deps=[tk] + kdeps)
                    pq_ring.used(qslot, tkc)
                    tk_qk = [tkc]
                tk3 = P.dma("sync", kr_d[:, t, :], qk[:, 512:1024], None, deps=tk_qk)
                qkrring.used(kslot, tk3)
                vr_, rdeps, rslot = vrsring.get()
                tke = []
                for half in range(2):
                    pb, pdeps, pslot = tmring.get()
                    for ko in range(8):
                        tk = P.mm(banks[pb][:, :], lhs(ko), w1[:, ko, 2560 + half * 512:3072 + half * 512],
                                  start=(ko == 0), stop=(ko == 7), deps=toks + [tk_w1] + pdeps, inc=(ko == 7))
                    if half == 0:
                        tk2 = P.cp("scalar", vr_[:, 0:512], banks[pb][:, :], deps=[tk] + rdeps)
                    else:
                        tk2 = P.cp("vector", vr_[:, 512:1024], banks[pb][:, :], deps=[tk] + rdeps)
                    tmring.used(pslot, tk2)
                    tke.append(tk2)
                hxring.used(hs, tk)
                tk3 = P.dma("sync", vr_d[:, t, :], vr_, None, deps=tke)
                vrsring.used(rslot, tk3)
                if not isctx:
                    pb2, pdeps2, pslot2 = tpring.get()
                    pt = bank_bf(pb2).rearrange("p (k n) -> p k n", k=8)
                    for k in range(8):
                        tk = P.tp(pt[:, k, :], qk[:, k * 128:(k + 1) * 128], identb[:, :], deps=tk_qk + pdeps2,
                                  inc=(k == 7))
                    qkrring.used(kslot, tk)
                    qT, tdeps, tslot = qkTring.get()
                    tk2 = P.cp("scalar", qT[:, :, :], pt, deps=[tk] + tdeps)
                    tpring.used(pslot2, tk2)
                    c0 = t * 128
                    tk3 = P.dma("sync", qr_d[:, :, c0:c0 + 128].rearrange("h p n -> p h n"), qT[:, 0:4, :], None,
                                deps=[tk2])
                    tk4 = P.dma("sync", krT_d[:, :, c0:c0 + 128].rearrange("h p n -> p h n"), qT[:, 4:8, :], None,
                                deps=[tk2])
                    qkTring.used(tslot, tk3)
                    qkTring.used(tslot, tk4)

        rtring = Ring([rtmp[:, 0, :], rtmp[:, 1, :]])
        load_x(0)
        load_x(1)
        prep_vec(0)
        prep_pe(0)
        for bi in range(nb):
            if bi + 2 < nb:
                load_x(bi + 2)
            fm(bi)
            if bi + 1 < nb:
                prep_vec(bi + 1, only=3)
                prep_pe(bi + 1)
            tm(bi)
        P.barrier()

    es_w1.close()
    if upto < 2:
        return finish(nc, P, y_d)
    es_wrg = ExitStack()
    wrg = es_wrg.enter_context(nc.sbuf_tensor("t_wrg", [128, 8, 1024], BF16, side="right"))
    s_wrg = P.newsem()
    tk_wrg = None
    for cb in range(2):
        tk_wrg = P.dma("gpsimd", wrg[:, :, cb * 512:(cb + 1) * 512],
                       w_in_d[:, 3584 + cb * 512:3584 + (cb + 1) * 512].rearrange("(k p) n -> p k n", p=128), s_wrg)
    with ExitStack() as es:
        kna_sb = es.enter_context(nc.sbuf_tensor("t_kna", [128, 4, NT], BF16))
        vna_sb = es.enter_context(nc.sbuf_tensor("t_vna", [128, 34, 520], BF16))
        qna_sb = es.enter_context(nc.sbuf_tensor("t_qna", [128, 4, N], BF16))
        btab = es.enter_context(nc.sbuf_tensor("t_btab", [128, NE, 8, 64], F32))
        mneg = es.enter_context(nc.sbuf_tensor("t_mneg", [128, NE, 64], F32))
        T_all = es.enter_context(nc.sbuf_tensor("t_T", [128, 2, 512], F32))
        PT_all = es.enter_context(nc.sbuf_tensor("t_PT", [128, 3, 512], BF16))
        rec_all = es.enter_context(nc.sbuf_tensor("t_rec", [128, 4, 8], F32))
        ona_all = es.enter_context(nc.sbuf_tensor("t_ona", [128, 2, 512], BF16))
        onaT_all = es.enter_context(nc.sbuf_tensor("t_onaT", [128, 2, 512], BF16))
        vsrc = vna_d[:, :].rearrange("(t p) e -> p t e", p=128)
        ksrc = kna_d[:, :, :].rearrange("c p n -> p c n")
        qsrc = qna_d[:, :, :].rearrange("c p n -> p c n")
        tk_b = P.dma("sync", btab[:, :, :, :].rearrange("p e h q -> p (e h q)"), rpbg_d[:, :])
        o2, n2 = COFF2["maskneg"]
        tk_m = P.dma("sync", mneg[:, :, :].rearrange("p e q -> p (e q)"), cst2_d[:, o2:o2 + n2])
        tk_kp, tk_vp, tk_qp = {}, {}, {}
        tk_kp[4] = P.dma("sync", kna_sb[:, :, N:NT], ksrc[:, :, N:NT])
        tk_vp[4] = P.dma("sync", vna_sb[:, 32:34, :], vsrc[:, 32:34, :])
        for pc in range(4):
            tk_kp[pc] = P.dma("sync", kna_sb[:, :, pc * 1024:(pc + 1) * 1024], ksrc[:, :, pc * 1024:(pc + 1) * 1024])
            tk_qp[pc] = P.dma("sync", qna_sb[:, :, pc * 1024:(pc + 1) * 1024], qsrc[:, :, pc * 1024:(pc + 1) * 1024])
            tk_vp[pc] = P.dma("sync", vna_sb[:, pc * 8:(pc + 1) * 8, :], vsrc[:, pc * 8:(pc + 1) * 8, :])
        tk_bt = None
        for e0 in range(0, NE, 4):
            e1 = min(NE, e0 + 4)
            tk_bt = P.tt("gpsimd", btab[:, e0:e1, :, :], btab[:, e0:e1, :, :],
                         mneg[:, e0:e1, :].unsqueeze(2).broadcast_to([128, e1 - e0, 8, 64]), ALU.add, deps=[tk_b, tk_m])
        Sring = Ring([0, 1, 2, 3])
        Oring = Ring([4, 5, 6])
        tpring = Ring([7])
        Tring = Ring([T_all[:, i, :] for i in range(2)])
        PTring = Ring([PT_all[:, i, :] for i in range(3)])
        onaring = Ring([ona_all[:, i, :] for i in range(2)])
        onaTring = Ring([onaT_all[:, i, :] for i in range(2)])
        units = []
        for qt in range(32):
            kts = [kt for kt in range(32) if any(NA_ENT(kt, 2 * qt + r) != (None, None) for r in range(2))]
            steps = [(kt, True) for kt in kts] + [(32, False), (33, False)]
            for si, (kt, loc) in enumerate(steps):
                for g in range(2):
                    units.append((qt, g, si, len(steps), kt, loc))
        state = {}

        def emit_S(u):
            qt, g, si, ns, kt, loc = u
            pb, pdeps, pslot = Sring.get()
            ps = g * 64
            for hh in range(4):
                tk = P.mm(banks[pb][:, hh * 128:(hh + 1) * 128], kna_sb[ps:ps + 64, hh, kt * 128:(kt + 1) * 128],
                          qna_sb[ps:ps + 64, hh, qt * 128:(qt + 1) * 128], start=True, stop=True,
                          deps=[tk_kp[kt // 8], tk_qp[qt // 8]] + pdeps, inc=(hh == 3))
            state[u] = (pb, pslot, tk)

        def emit_rest(u):
            qt, g, si, ns, kt, loc = u
            pb, pslot, tk_s = state.pop(u)
            Sv = banks[pb][:, :].rearrange("p (h q) -> p h q", h=4)
            pt, ptdeps, ptslot = PTring.get()
            ptv = pt.rearrange("p (h q) -> p h q", h=4)
            if loc:
                Tt, tdeps, tslot = Tring.get()
                Tv = Tt.rearrange("p (h q) -> p h q", h=4)
                for r in range(2):
                    e = NA_ENTS[NA_ENT(kt, 2 * qt + r)]
                    tk_a = P.tt("vector", Tv[:, :, r * 64:(r + 1) * 64], Sv[:, :, r * 64:(r + 1) * 64],
                                btab[:, e, g:8:2, :], ALU.add, deps=[tk_s, tk_bt] + tdeps)
                Sring.used(pslot, tk_a)
                tk_e = P.act(pt, Tt, AF.Exp, deps=[tk_a] + ptdeps)
                Tring.used(tslot, tk_e)
            else:
                tk_e = P.act(pt, banks[pb][:, :], AF.Exp, deps=[tk_s] + ptdeps)
                Sring.used(pslot, tk_e)
            if si == 0:
                ob, odeps, oslot = Oring.get()
                state[("O", qt, g)] = (ob, oslot)
            else:
                ob, oslot = state[("O", qt, g)]
                odeps = []
            for hh in range(4):
                h = 2 * hh + g
                first = (si == 0 and hh == 0)
                tk = P.op("tensor", lambda e, o=banks[ob][:, hh * 65:(hh + 1) * 65], l=ptv[:, hh, :],
                          r=vna_sb[:, kt, h * 65:(h + 1) * 65], st=first, sp=(si == ns - 1):
                          e.matmul(o, l, r, start=st, stop=sp, skip_group_check=True),
                          deps=[tk_e, tk_vp[kt // 8]] + odeps, inc=(hh == 3))
            PTring.used(ptslot, tk)
            if si == ns - 1:
                Ov = banks[ob][:, 0:260].rearrange("p (h e) -> p h e", e=65)
                if g == 0:
                    on_, ondeps, onslot = onaring.get()
                    state[("ona", qt)] = (on_, onslot)
                else:
                    on_, onslot = state[("ona", qt)]
                    ondeps = []
                rec = rec_all[:, qt % 4, g * 4:(g + 1) * 4]
                tk_r = P.recip(rec, Ov[:, :, 64], deps=[tk])
                onv = on_.rearrange("p (h e) -> p h e", e=64)
                tk_n = P.tt("vector", onv[:, g:8:2, :], Ov[:, :, 0:64], rec.unsqueeze(2).broadcast_to([128, 4, 64]),
                            ALU.mult, deps=[tk_r] + ondeps)
                Oring.used(oslot, tk_n)
                state[("n", qt, g)] = tk_n
                if g == 1:
                    pb2, pdeps2, pslot2 = tpring.get()
                    ptT = bank_bf(pb2)[:, 0:512].rearrange("p (c n) -> p c n", c=4)
                    for c in range(4):
                        tk = P.tp(ptT[:, c, :], on_[:, c * 128:(c + 1) * 128], identb[:, :],
                                  deps=[state[("n", qt, 0)], tk_n] + pdeps2, inc=(c == 3))
                    onaring.used(onslot, tk)
                    oT_, otdeps, otslot = onaTring.get()
                    tk2 = P.cp("scalar", oT_.rearrange("p (c n) -> p c n", c=4), ptT, deps=[tk] + otdeps)
                    tpring.used(pslot2, tk2)
                    tk3 = P.dma("sync", oT_d[0:4, :, qt * 128:(qt + 1) * 128].rearrange("c p n -> p c n"),
                                oT_.rearrange("p (c n) -> p c n", c=4), None, deps=[tk2])
                    onaTring.used(otslot, tk3)

        emit_S(units[0])
        emit_S(units[1])
        for i in range(0, len(units), 2):
            if i + 2 < len(units):
                emit_S(units[i + 2])
                emit_S(units[i + 3])
            emit_rest(units[i])
            emit_rest(units[i + 1])
        P.barrier()

    if upto < 3:
        return finish(nc, P, y_d)

    es_w3 = ExitStack()
    wg = es_w3.enter_context(nc.sbuf_tensor("t3_wg", [128, 8, 2048], BF16))
    wpn = es_w3.enter_context(nc.sbuf_tensor("t3_wpn", [128, 4, 1024], BF16))
    wpr = es_w3.enter_context(nc.sbuf_tensor("t3_wpr", [128, 8, 1024], BF16))
    wo = es_w3.enter_context(nc.sbuf_tensor("t3_wo", [128, 8, 1024], BF16))
    s_w3 = P.newsem()
    tk_w3 = None
    for cb in range(4):
        tk_w3 = P.dma("gpsimd", wg[:, :, cb * 512:(cb + 1) * 512],
                      w_in_d[:, 4608 + cb * 512:4608 + (cb + 1) * 512].rearrange("(k p) n -> p k n", p=128), s_w3)
    for cb in range(2):
        tk_w3 = P.dma("gpsimd", wpn[:, :, cb * 512:(cb + 1) * 512],
                      w_pna_d[:, cb * 512:(cb + 1) * 512].rearrange("(k p) n -> p k n", p=128), s_w3)
        tk_w3 = P.dma("gpsimd", wpr[:, :, cb * 512:(cb + 1) * 512],
                      w_pret_d[:, cb * 512:(cb + 1) * 512].rearrange("(k p) n -> p k n", p=128), s_w3)
        tk_w3 = P.dma("gpsimd", wo[:, :, cb * 512:(cb + 1) * 512],
                      w_out_d[:, cb * 512:(cb + 1) * 512].rearrange("(k p) n -> p k n", p=128), s_w3)
    with ExitStack() as es:
        c2 = es.enter_context(nc.sbuf_tensor("t_c2", [128, 768], F32))
        dmT = es.enter_context(nc.sbuf_tensor("t_dmT", [128, 4, 128], F32))
        dtmp = es.enter_context(nc.sbuf_tensor("t_dtmp", [128, 8, 128], F32))
        xi_all = es.enter_context(nc.sbuf_tensor("t_xi", [128, 2, 512], BF16))
        zg = es.enter_context(nc.sbuf_tensor("t_zg", [128, 4, 4], F32))
        S_all = es.enter_context(nc.sbuf_tensor("t_S", [128, 4, 1024], F32))
        Sbf_all = es.enter_context(nc.sbuf_tensor("t_Sbf", [128, 3, 1024], BF16))
        K_all = es.enter_context(nc.sbuf_tensor("t_K", [128, 2, KVG * 512], BF16))
        V_all = es.enter_context(nc.sbuf_tensor("t_V", [128, 2, KVG * 1024], BF16))
        qT_all = es.enter_context(nc.sbuf_tensor("t_qT", [128, 2, 512], BF16))
        kT_all = es.enter_context(nc.sbuf_tensor("t_kT", [128, 2, 512], BF16))
        sbi_all = es.enter_context(nc.sbuf_tensor("t_sbi", [128, 2, 1024], BF16))
        hxi_all = es.enter_context(nc.sbuf_tensor("t_hxi", [128, 2, 1024], BF16))
        Kz_all = es.enter_context(nc.sbuf_tensor("t_Kz", [128, 2, 512], BF16))
        PTr_all = es.enter_context(nc.sbuf_tensor("t_PTr", [128, 2, 512], BF16))
        qx_all = es.enter_context(nc.sbuf_tensor("t_qx", [128, 4, 512], BF16))
        on_all = es.enter_context(nc.sbuf_tensor("t_on", [128, 2, 1024], F32))
        sg_all = es.enter_context(nc.sbuf_tensor("t_sg", [128, 2, 1024], F32))
        oret_all = es.enter_context(nc.sbuf_tensor("t_oret", [128, 2, 1024], BF16))
        oretT_all = es.enter_context(nc.sbuf_tensor("t_oretT", [128, 2, 1024], BF16))
        st_all = es.enter_context(nc.sbuf_tensor("t_st", [128, 32, 40], F32))
        tk_c2 = P.dma("sync", c2[:, :], cst2_d[:, 0:768])

        def C2(name):
            o, n = COFF2[name]
            return c2[:, o:o + n]
        lgf, lgb = C("lgf"), C("lgb")
        zf, zb, gLf, gLb = zg[:, 0, :], zg[:, 1, :], zg[:, 2, :], zg[:, 3, :]
        xif = xi_all[:, 0, :].rearrange("p (h n) -> p h n", h=4)
        xib = xi_all[:, 1, :].rearrange("p (h n) -> p h n", h=4)
        tks_tab = []
        for h in range(4):
            a1_ = P.act(dtmp[:, 2 * h, :], C2("dpos"), AF.Exp, scale=lgf[:, h:h + 1], deps=[tk_c2, tk_cst])
            a2_ = P.act(dtmp[:, 2 * h + 1, :], C2("dneg"), AF.Exp, scale=lgb[:, h:h + 1], deps=[tk_c2])
            b1_ = P.tt("vector", dtmp[:, 2 * h, :], dtmp[:, 2 * h, :], C2("mf"), ALU.mult, deps=[a1_])
            b2_ = P.tt("vector", dtmp[:, 2 * h + 1, :], dtmp[:, 2 * h + 1, :], C2("mb"), ALU.mult, deps=[a2_])
            b3_ = P.tt("vector", dmT[:, h, :], dtmp[:, 2 * h, :], dtmp[:, 2 * h + 1, :], ALU.add, deps=[b1_, b2_])
            b4_ = P.ts("vector", dmT[:, h, :], dmT[:, h, :], RK_SCALE, None, ALU.mult, deps=[b3_])
            a3_ = P.act(xif[:, h, :], C2("lp1"), AF.Exp, scale=lgf[:, h:h + 1], deps=[b4_])
            a4_ = P.act(xib[:, h, :], C2("lml"), AF.Exp, scale=lgb[:, h:h + 1], deps=[b4_])
            a5_ = P.act(zf[:, h:h + 1], C("cm1"), AF.Exp, scale=lgf[:, h:h + 1], deps=[tk_cst])
            a6_ = P.act(zb[:, h:h + 1], C("cm"), AF.Exp, scale=lgb[:, h:h + 1], deps=[tk_cst])
            tks_tab += [b4_, a3_, a4_, a5_, a6_]
        a7_ = P.act(gLf, lgf, AF.Exp, scale=128.0, deps=[tk_cst])
        a8_ = P.act(gLb, lgb, AF.Exp, scale=128.0, deps=[tk_cst])
        z1_ = P.ts("vector", zg[:, 0:2, :], zg[:, 0:2, :], RK_SCALE, None, ALU.mult, deps=tks_tab + [a7_, a8_])
        tk_tab = [z1_, a7_, a8_] + tks_tab

        Kring = Ring([K_all[:, i, :] for i in range(2)])
        Vring = Ring([V_all[:, i, :] for i in range(2)])
        kv_groups = {}
        qTring = Ring([qT_all[:, i, :] for i in range(2)])
        kTring = Ring([kT_all[:, i, :] for i in range(2)])
        sbiring = Ring([sbi_all[:, i, :] for i in range(2)])
        hxiring = Ring([hxi_all[:, i, :] for i in range(2)])
        Kzring = Ring([Kz_all[:, i, :] for i in range(2)])
        PTrring = Ring([PTr_all[:, i, :] for i in range(2)])
        qxring = Ring([(qx_all[:, 2 * i, :], qx_all[:, 2 * i + 1, :]) for i in range(2)])
        onring = Ring([on_all[:, i, :] for i in range(2)])
        sgring = Ring([sg_all[:, i, :] for i in range(2)])
        oretring = Ring([oret_all[:, i, :] for i in range(2)])
        oretTring = Ring([oretT_all[:, i, :] for i in range(2)])
        Sbfring = Ring([Sbf_all[:, i, :] for i in range(3)])
        Pst_rings = {"bwd": Ring([3, 5]), "fwd": Ring([3])}
        pst_mode = ["bwd"]
        Sbuf = {"f": [S_all[:, 0, :], S_all[:, 1, :]], "b": [S_all[:, 2, :], S_all[:, 3, :]]}
        tk_S0 = [P.memset("vector", Sbuf["f"][0], 0.0), P.memset("vector", Sbuf["b"][0], 0.0)]
        Pout = psum_all[:, 1 * 512:3 * 512]
        Pg = psum_all[:, 5 * 512:7 * 512]
        Sstate = {w: {"cur": 0, "tok": list(tk_S0), "casts": [[], []]} for w in ("f", "b")}

        def load_KV(t):
            g = t // KVG
            if g not in kv_groups:
                Kg, kd, ks = Kring.get()
                Vg, vd, vs_ = Vring.get()
                for g_old in [go for go, v in kv_groups.items() if v[1] == ks]:
                    del kv_groups[g_old]
                nt_ = min(KVG, 34 - KVG * g)
                tkk = P.dma("sync", Kg[:, 0:nt_ * 512].rearrange("p (t f) -> p t f", t=nt_), kr_d[:, KVG * g:KVG * g + nt_, :],
                            None, deps=kd)
                tkv = P.dma("sync", Vg[:, 0:nt_ * 1024].rearrange("p (t f) -> p t f", t=nt_), vr_d[:, KVG * g:KVG * g + nt_, :],
                            None, deps=vd)
                kv_groups[g] = (Kg, ks, tkk, Vg, vs_, tkv)
            Kg, ks, tkk, Vg, vs_, tkv = kv_groups[g]
            o = t % KVG
            return (Kg[:, o * 512:(o + 1) * 512], ks, tkk), (Vg[:, o * 1024:(o + 1) * 1024], vs_, tkv)

        def kz_prep(which, KV):
            (Kt, ks, tkk), _ = KV
            z = zf if which == "f" else zb
            Kz, zdeps, zslot = Kzring.get()
            if pst_mode[0] == "bwd":
                tk1 = P.tt("gpsimd", Kz.rearrange("p (h d) -> p h d", h=4), Kt.rearrange("p (h d) -> p h d", h=4),
                           z.unsqueeze(2).broadcast_to([128, 4, 128]), ALU.mult, deps=[tkk] + tk_tab + zdeps)
            else:
                for h in range(4):
                    tk1 = P.act(Kz[:, h * 128:(h + 1) * 128], Kt[:, h * 128:(h + 1) * 128], AF.Copy,
                                scale=z[:, h:h + 1], deps=[tkk] + tk_tab + zdeps)
            Kring.used(ks, tk1)
            return Kz, zslot, tk1

        def state_step(which, KV, kzp=None):
            (Kt, ks, tkk), (Vt, vs_, tkv) = KV
            z = zf if which == "f" else zb
            gL = gLf if which == "f" else gLb
            stt_ = Sstate[which]
            cur = stt_["cur"]
            nxt = 1 - cur
            S, Sn = Sbuf[which][cur], Sbuf[which][nxt]
            if kzp is None:
                kzp = kz_prep(which, KV)
            Kz, zslot, tk1 = kzp
            ring = Pst_rings[pst_mode[0]]
            b0, pdeps, pslot = ring.get()
            Pst = psum_all[:, b0 * 512:(b0 + 2) * 512]
            for h in range(4):
                tk2 = P.mm(Pst[:, h * 256:(h + 1) * 256], Kz[:, h * 128:(h + 1) * 128], Vt[:, h * 256:(h + 1) * 256],
                           start=True, stop=True, deps=[tk1, tkv] + pdeps, inc=(h == 3))
            Kzring.used(zslot, tk2)
            Vring.used(vs_, tk2)
            deps_s = list(stt_["tok"]) + list(stt_["casts"][nxt])
            for h in range(4):
                tk3 = P.stt(Sn[:, h * 256:(h + 1) * 256], S[:, h * 256:(h + 1) * 256], gL[:, h:h + 1],
                            Pst[:, h * 256:(h + 1) * 256], ALU.mult, ALU.add, deps=[tk2] + deps_s + tk_tab)
            ring.used(pslot, tk3)
            stt_["tok"] = [tk3]
            stt_["cur"] = nxt
            stt_["casts"][nxt] = []

        def cast_state(which):
            stt_ = Sstate[which]
            S = Sbuf[which][stt_["cur"]]
            sbf, sdeps, sslot = Sbfring.get()
            tk = P.cp("scalar", sbf, S, deps=list(stt_["tok"]) + sdeps)
            stt_["casts"][stt_["cur"]].append(tk)
            return sbf, sslot, tk

        kv0 = load_KV(32)
        kv1 = load_KV(33)
        state_step("f", kv0)
        kv0b = load_KV(32)
        state_step("f", kv1)
        kv1b = load_KV(33)
        state_step("b", kv1b)
        state_step("b", kv0b)
        sb_toks = {}
        kvq = {}
        for i in range(31, -1, -1):
            for j in (i, i - 1, i - 2):
                if j >= 1 and j not in kvq:
                    kvq[j] = load_KV(j)
            sbf, sslot, tkc = cast_state("b")
            tks = P.dma("sync", sb_d[i, :, :], sbf, None, deps=[tkc])
            sb_toks[i] = tks
            Sbfring.used(sslot, tks)
            if i > 0:
                state_step("b", kvq.pop(i))
        pst_mode[0] = "fwd"
        g_guard0 = list(Sstate["b"]["tok"])

        def load_main(i):
            c0 = i * 128
            q_, qd, qs = qTring.get()
            k_, kd, ks = kTring.get()
            s_, sd, ss_ = sbiring.get()
            h_, hd, hs = hxiring.get()
            t1 = P.dma("sync", q_.rearrange("p (h n) -> p h n", h=4), qr_d[:, :, c0:c0 + 128].rearrange("h p n -> p h n"), None, deps=qd)
            t2 = P.dma("sync", k_.rearrange("p (h n) -> p h n", h=4), krT_d[:, :, c0:c0 + 128].rearrange("h p n -> p h n"), None, deps=kd)
            t3 = P.dma("sync", s_, sb_d[i, :, :], None, deps=sd + [sb_toks[i]])
            t4 = P.dma("sync", h_.rearrange("p (k n) -> p k n", k=8), hxT_d[:, :, c0:c0 + 128].rearrange("k p n -> p k n"), None, deps=hd)
            return (q_, qs, t1), (k_, ks, t2), (s_, ss_, t3), (h_, hs, t4), load_KV(i)

        gp_pre_of = {}

        def gp_pre(i, L):
            (q_, qs, t1), _, _, _, KV = L
            (qxf, qxb), xdeps, xslot = qxring.get()
            tk_x1 = P.tt("gpsimd", qxf, q_, xi_all[:, 0, :], ALU.mult, deps=[t1] + tk_tab + xdeps)
            tk_x2 = P.tt("gpsimd", qxb, q_, xi_all[:, 1, :], ALU.mult, deps=[t1])
            qTring.used(qs, tk_x2)
            kzp = kz_prep("f", KV) if i < 31 else None
            gp_pre_of[i] = ((qxf, qxb), xslot, tk_x1, tk_x2, kzp)

        def main(i, L, Lnext):
            (q_, qs, t1), (k_, ks, t2), (s_, ss_, t3), (h_, hs, t4), KV = L
            (Kt, kks, tkk), (Vt, vs_, tkv) = KV
            c0 = i * 128
            qv = q_.rearrange("p (h n) -> p h n", h=4)
            kv = k_.rearrange("p (h n) -> p h n", h=4)
            sfbf, sfslot, tk_sf = cast_state("f")
            (qxf, qxb), xslot, tk_x1, tk_x2, kzp = gp_pre_of.pop(i)
            if i < 31:
                state_step("f", KV, kzp)
            for h in range(4):
                tk = P.mm(banks[0][:, h * 128:(h + 1) * 128], kv[:, h, :], qv[:, h, :], start=True, stop=True,
                          deps=[t1, t2] + sc_guard, inc=(h == 3))
            kTring.used(ks, tk)
            ptr, pdeps, pslot = PTrring.get()
            tk_p = P.tt("vector", ptr, banks[0][:, :], dmT[:, :, :].rearrange("p h n -> p (h n)"), ALU.mult,
                        deps=[tk] + tk_tab + pdeps)
            sc_guard.clear()
            sc_guard.append(tk_p)
            qTring.used(qs, tk)
            hv = h_.rearrange("p (k n) -> p k n", k=8)
            for half in range(2):
                for ko in range(8):
                    tk_g = P.mm(Pg[:, half * 512:(half + 1) * 512], hv[:, ko, :], wrg[:, ko, half * 512:(half + 1) * 512],
                                start=(ko == 0), stop=(ko == 7), deps=[t4, tk_wrg] + g_guard, inc=(ko == 7 and half == 1))
            hxiring.used(hs, tk_g)
            sg, sgd, sgs = sgring.get()
            tk_sg = P.act(sg, Pg, AF.Silu, deps=[tk_g] + sgd)
            g_guard.clear()
            g_guard.append(tk_sg)
            for h in range(4):
                sl = slice(h * 256, (h + 1) * 256)
                P.mm(Pout[:, sl], ptr[:, h * 128:(h + 1) * 128], Vt[:, sl], start=True, stop=False,
                     deps=[tk_p, tkv] + out_guard)
                P.mm(Pout[:, sl], qxf[:, h * 128:(h + 1) * 128], sfbf[:, sl], start=False, stop=False, deps=[tk_x1, tk_sf])
                tk_o = P.mm(Pout[:, sl], qxb[:, h * 128:(h + 1) * 128], s_[:, sl], start=False, stop=True,
                            deps=[tk_x2, t3], inc=(h == 3))
            PTrring.used(pslot, tk_o)
            qxring.used(xslot, tk_o)
            sbiring.used(ss_, tk_o)
            Sbfring.used(sfslot, tk_o)
            st = st_all[:, i, :]
            bn = st[:, 0:24].rearrange("p (h s) -> p h s", h=4)
            mv = st[:, 24:32].rearrange("p (h s) -> p h s", h=4)
            rt = st[:, 32:36]
            rstd = st[:, 36:40]
            for h in range(4):
                tk_b1 = P.op("vector", lambda e, o=bn[:, h, :], a=Pout[:, h * 256:(h + 1) * 256]: e.bn_stats(o, a), deps=[tk_o])
                tk_b2 = P.op("vector", lambda e, o=mv[:, h, :], a=bn[:, h, :]: e.bn_aggr(o, a), deps=[tk_b1])
            tk_rt = P.ts("gpsimd", rt, mv[:, :, 1], 1.0, EPS, ALU.mult, ALU.add, deps=[tk_b2])
            tk_rs = P.tt("gpsimd", rstd, rt, mhalf[:, 0:4], ALU.pow, deps=[tk_rt, tk_mh])
            if Lnext is not None:
                gp_pre(i + 1, Lnext)
            on_, ond, ons = onring.get()
            for h in range(4):
                tk_n = P.ts("vector", on_[:, h * 256:(h + 1) * 256], Pout[:, h * 256:(h + 1) * 256], mv[:, h, 0:1],
                            rstd[:, h:h + 1], ALU.subtract, ALU.mult, deps=[tk_rs] + ond)
            out_guard.clear()
            out_guard.append(tk_n)
            orr, ord_, ors = oretring.get()
            tk_m = P.tt("gpsimd", orr, on_, sg, ALU.mult, deps=[tk_n, tk_sg] + ord_)
            onring.used(ons, tk_m)
            sgring.used(sgs, tk_m)
            Kring.used(kks, tk_o)
            Vring.used(vs_, tk_o)

            def part2():
                main_tail(i, c0, orr, ors, tk_m)
            return part2

        def main_tail(i, c0, orr, ors, tk_m):
            pb2 = 7
            ptT = bank_bf(pb2).rearrange("p (c n) -> p c n", c=8)
            for c in range(8):
                tk_t = P.tp(ptT[:, c, :], orr[:, c * 128:(c + 1) * 128], identb[:, :], deps=[tk_m] + tp_guard, inc=(c == 7))
            oretring.used(ors, tk_t)
            oT_, otd, ots = oretTring.get()
            tk_e = P.cp("scalar", oT_, bank_bf(pb2), deps=[tk_t] + otd)
            tp_guard.clear()
            tp_guard.append(tk_e)
            tk_st = P.dma("sync", oT_d[4:12, :, c0:c0 + 128].rearrange("c p n -> p c n"),
                          oT_.rearrange("p (c n) -> p c n", c=8), None, deps=[tk_e])
            oretTring.used(ots, tk_st)

        sc_guard, out_guard, g_guard, tp_guard = [], [], [], []
        g_guard.extend(g_guard0)
        kv_groups.clear()
        L = load_main(0)
        gp_pre(0, L)
        pend = None
        for i in range(32):
            Ln = load_main(i + 1) if i < 31 else None
            nxt = main(i, L, Ln)
            if pend is not None:
                pend()
            pend = nxt
            L = Ln
        pend()
        P.barrier()

    es_wrg.close()
    if upto < 4:
        return finish(nc, P, y_d)
    es_wp0 = ExitStack()
    wu_p0 = es_wp0.enter_context(nc.sbuf_tensor("t4_wup0", [128, 8, 1024], BF16, side="right"))
    s_wp0 = P.newsem()
    tk_wp0 = None
    for part in range(2):
        tk_wp0 = P.dma("gpsimd", wu_p0[:, :, part * 512:(part + 1) * 512],
                       w_up_d[:, part * DFF:part * DFF + 512].rearrange("(k p) n -> p k n", p=128), s_wp0)
    TB = 256
    with ExitStack() as es:
        def T_(name, shape, dt):
            return es.enter_context(nc.sbuf_tensor("t3_" + name, shape, dt))
        hxb_all = T_("hxb", [128, 2, 8 * TB], BF16)
        oTb_all = T_("oTb", [128, 2, 12 * TB], BF16)
        mT_all = T_("mT", [128, 2, 8 * TB], BF16)
        sg_all3 = T_("sg", [128, 4, TB], F32)
        m_all = T_("m", [128, 4, TB], F32)
        xt3_all = T_("xt", [128, 6, 1024], F32)
        tmp3_all = T_("tmp", [128, 2, 1024], F32)
        x1_all = T_("x1", [128, 2, 1024], F32)
        xn2_all = T_("xn2", [128, 4, 1024], BF16)
        h2_all = T_("h2", [128, 2, 1024], BF16)
        junk3 = T_("junk", [128, 1024], BF16)
        zer = T_("zer", [128, 8], BF16)
        tk_z = P.memset("vector", zer[:, :], 0.0)
        P.dma("sync", hx2T_d[:, :, 0:1].rearrange("k p n -> p k n"), zer[:, :].unsqueeze(2), None, deps=[tk_z], allow_slow_non_contiguous=True)
        P.dma("sync", hx2T_d[:, :, N + 1:N + 2].rearrange("k p n -> p k n"), zer[:, :].unsqueeze(2), None, deps=[tk_z], allow_slow_non_contiguous=True)
        hxbring = Ring([hxb_all[:, i, :].rearrange("p (k n) -> p k n", k=8) for i in range(2)])
        oTbring = Ring([oTb_all[:, i, :].rearrange("p (k n) -> p k n", k=12) for i in range(2)])
        mTring = Ring([mT_all[:, i, :].rearrange("p (k n) -> p k n", k=8) for i in range(2)])
        sgring3 = Ring([sg_all3[:, i, :] for i in range(4)])
        mring = Ring([m_all[:, i, :] for i in range(4)])
        xt3ring = Ring([xt3_all[:, i, :] for i in range(6)])
        tmp3ring = Ring([tmp3_all[:, i, :] for i in range(2)])
        x1ring = Ring([x1_all[:, i, :] for i in range(2)])
        xn2ring = Ring([xn2_all[:, i, :] for i in range(4)])
        h2ring = Ring([h2_all[:, i, :].rearrange("p (k n) -> p k n", k=8) for i in range(2)])
        pring = Ring([0, 1, 2, 3])
        poring = Ring([4, 5])
        tpring3 = Ring([6, 7])
        ss3 = smalls[:, 64:128]
        rt3 = smalls[:, 128:192]
        rs3 = smalls[:, 192:256]
        nblk = N // TB

        def load_blk(b):
            t0 = b * TB
            hb, hd, hs = hxbring.get()
            ob, od, os_ = oTbring.get()
            tk1 = P.dma("sync", hb, hxT_d[:, :, t0:t0 + TB].rearrange("k p n -> p k n"), None, deps=hd)
            tk2 = P.dma("sync", ob, oT_d[:, :, t0:t0 + TB].rearrange("k p n -> p k n"), None, deps=od)
            xs = []
            for i in range(TB // 128):
                xt, xd, xsl = xt3ring.get()
                tkx = P.dma("sync", xt, x_d[t0 + i * 128:t0 + (i + 1) * 128, :], None, deps=xd)
                xs.append((xt, xsl, tkx))
            return (hb, hs, tk1), (ob, os_, tk2), xs

        def proc_blk(b, L):
            (hb, hs, tk1), (ob, os_, tk2), xs = L
            t0 = b * TB
            mT, md, ms = mTring.get()
            last_mm = None
            for fc in range(8):
                fsl = slice(fc * 128, (fc + 1) * 128)
                res = {}
                for which in (1, 3, 0, 2):
                    pb, pd, psl = pring.get()
                    if which == 0:
                        nk, lw, rh = 4, (lambda k: wpn[:, k, fsl]), (lambda k: ob[:, k, :])
                    elif which == 2:
                        nk, lw, rh = 8, (lambda k: wpr[:, k, fsl]), (lambda k: ob[:, 4 + k, :])
                    elif which == 1:
                        nk, lw, rh = 8, (lambda k: wg[:, k, fc * 128:(fc + 1) * 128]), (lambda k: hb[:, k, :])
                    else:
                        nk, lw, rh = 8, (lambda k: wg[:, k, 1024 + fc * 128:1024 + (fc + 1) * 128]), (lambda k: hb[:, k, :])
                    for k in range(nk):
                        tk = P.mm(banks[pb][:, 0:TB], lw(k), rh(k), start=(k == 0), stop=(k == nk - 1),
                                  deps=[tk1, tk2, tk_w3] + pd, inc=(k == nk - 1))
                    res[which] = (pb, psl, tk)
                    last_mm = tk
                sga, sad, sas = sgring3.get()
                tk_sa = P.act(sga, banks[res[1][0]][:, 0:TB], AF.Sigmoid, deps=[res[1][2]] + sad)
                pring.used(res[1][1], tk_sa)
                sgb, sbd, sbs = sgring3.get()
                tk_sb = P.act(sgb, banks[res[3][0]][:, 0:TB], AF.Sigmoid, deps=[res[3][2]] + sbd)
                pring.used(res[3][1], tk_sb)
                m1, m1d, m1s = mring.get()
                tk_m1 = P.tt("vector", m1, banks[res[0][0]][:, 0:TB], sga, ALU.mult, deps=[res[0][2], tk_sa] + m1d)
                pring.used(res[0][1], tk_m1)
                sgring3.used(sas, tk_m1)
                m2, m2d, m2s = mring.get()
                tk_m2 = P.tt("vector", m2, banks[res[2][0]][:, 0:TB], sgb, ALU.mult, deps=[res[2][2], tk_sb] + m2d)
                pring.used(res[2][1], tk_m2)
                sgring3.used(sbs, tk_m2)
                tk_mt = P.tt("gpsimd", mT[:, fc, :], m1, m2, ALU.add, deps=[tk_m1, tk_m2] + md)
                mring.used(m1s, tk_mt)
                mring.used(m2s, tk_mt)
            hxbring.used(hs, last_mm)
            oTbring.used(os_, last_mm)

            def part2():
                pend_tiles = []
                for i in range(TB // 128):
                    xt, xsl, tkx = xs[i]
                    t = (t0 // 128) + i
                    x1, x1d, x1s = x1ring.get()
                    tmp, tmd, tms = tmp3ring.get()
                    for nh in range(2):
                        pb, pd, psl = poring.get()
                        for fc in range(8):
                            tk = P.mm(banks[pb][:, :], mT[:, fc, i * 128:(i + 1) * 128], wo[:, fc, nh * 512:(nh + 1) * 512],
                                      start=(fc == 0), stop=(fc == 7), deps=[tk_mt, tk_w3] + pd, inc=(fc == 7))
                        tk_a = P.tt("vector", tmp[:, nh * 512:(nh + 1) * 512], banks[pb][:, :], gbc[:, nh * 512:(nh + 1) * 512],
                                    ALU.mult, deps=[tk] + tmd)
                        poring.used(psl, tk_a)
                        tk_b = P.tt("gpsimd", x1[:, nh * 512:(nh + 1) * 512], tmp[:, nh * 512:(nh + 1) * 512],
                                    xt[:, nh * 512:(nh + 1) * 512], ALU.add, deps=[tk_a, tkx] + x1d)
                    mTring.used(ms, tk)
                    tmp3ring.used(tms, tk_b)
                    xt3ring.used(xsl, tk_b)
                    tk_st = P.dma("sync", x1_d[t * 128:(t + 1) * 128, :], x1, None, deps=[tk_b])
                    x1ring.used(x1s, tk_st)
                    tk_ss = P.act(junk3[:, :], x1, AF.Square, accum_out=ss3[:, t:t + 1], deps=[tk_b])
                    tk_rt = P.ts("gpsimd", rt3[:, t:t + 1], ss3[:, t:t + 1], 1.0 / D, EPS, ALU.mult, ALU.add, deps=[tk_ss])
                    tk_r = P.tt("gpsimd", rs3[:, t:t + 1], rt3[:, t:t + 1], mhalf[:, 0:1], ALU.pow, deps=[tk_rt, tk_mh])
                    xn, nd, ns_ = xn2ring.get()
                    tk_xn = P.act(xn, x1, AF.Copy, scale=rs3[:, t:t + 1], deps=[tk_r] + nd)
                    x1ring.used(x1s, tk_xn)
                    pend_tiles.append((xn, ns_, tk_xn, t))

                def part2b():
                    for (xn, ns_, tk_xn, t) in pend_tiles:
                        pb2, pd2, psl2 = tpring3.get()
                        pt = bank_bf(pb2).rearrange("p (k n) -> p k n", k=8)
                        for k in range(8):
                            tk = P.tp(pt[:, k, :], xn[:, k * 128:(k + 1) * 128], identb[:, :], deps=[tk_xn] + pd2, inc=(k == 7))
                        xn2ring.used(ns_, tk)
                        h2, h2d, h2s = h2ring.get()
                        for k in range(8):
                            tk2_ = P.ts("vector", h2[:, k, :], pt[:, k, :], modv[:, 4, k:k + 1], modv[:, 5, k:k + 1],
                                        ALU.mult, ALU.add, deps=[tk] + h2d)
                        tpring3.used(psl2, tk2_)
                        tk_s2 = P.dma("sync", hx2T_d[:, :, 1 + t * 128:1 + (t + 1) * 128].rearrange("k p n -> p k n"), h2, None,
                                      deps=[tk2_])
                        h2ring.used(h2s, tk_s2)

                return part2b
            return part2

        L = load_blk(0)
        pend = None
        pendb = None
        for b in range(nblk):
            Ln = load_blk(b + 1) if b + 1 < nblk else None
            nxt = proc_blk(b, L)
            nb = pend() if pend is not None else None
            if pendb is not None:
                pendb()
            pendb = nb
            pend = nxt
            L = Ln
        nb = pend()
        if pendb is not None:
            pendb()
        nb()
        P.barrier()

    es_w3.close()
    if upto < 5:
        return finish(nc, P, y_d)

    hT_d = nc.dram_tensor("hT_lo", [11, 128, N], BF16, kind="Internal").ap()
    FB = 256
    es_w4b = ExitStack()
    wu_n = es_w4b.enter_context(nc.sbuf_tensor("t4n_wu", [128, 8, 22 * 128], BF16))
    wd_n = es_w4b.enter_context(nc.sbuf_tensor("t4n_wd", [128, 22, 1024], BF16))
    for pas in range(2):
        with ExitStack() as es:
            def T_(name, shape, dt):
                return es.enter_context(nc.sbuf_tensor("t4%d_" % pas + name, shape, dt))
            if pas == 0:
                wu = T_("wu", [128, 8, 22 * 128], BF16)
                wd = None
            else:
                wu, wd = wu_n, wd_n
            h2b_all = T_("h2b", [128, 2, 8 * (FB + 2)], BF16)
            ct_all = T_("ct", [128, 2, 7 * FB], F32)
            hT_all = T_("hT", [128, 2, 11 * FB], BF16)
            hTlo_all = T_("hTlo", [128, 3, 11 * FB], BF16) if pas == 1 else None
            if pas == 1:
                x1b_all = T_("x1b", [128, 4, 1024], F32)
                tmp4_all = T_("tmp4", [128, 1, 1024], F32)
                x2_all = T_("x2", [128, 2, 1024], F32)
                junk4 = T_("junk4", [128, 1024], BF16)
            def load_wu(dst, pas_, sems):
                toks_ = []
                for pi, cb in enumerate(range(0, 11 * 128, 512)):
                    w_ = min(512, 11 * 128 - cb)
                    tk_ = None
                    if pas_ == 0 and pi == 0:
                        toks_.append(tk_wp0)
                        continue
                    for part in range(2):
                        c0 = part * DFF + pas_ * 11 * 128
                        tk_ = P.dma("gpsimd", dst[:, :, part * 1408 + cb:part * 1408 + cb + w_],
                                    w_up_d[:, c0 + cb:c0 + cb + w_].rearrange("(k p) n -> p k n", p=128), sems[pi])
                    toks_.append(tk_)
                return toks_
            if pas == 0:
                s_w4a = [P.newsem(), P.newsem(), P.newsem()]
                s_w4b = P.newsem()
                tk_w4p = load_wu(wu, 0, s_w4a)
                tk_w4 = tk_w4p[-1]
                tk_w4n = load_wu(wu_n, 1, [s_w4b] * 3)[-1]
                for cb in range(2):
                    tk_w4n = P.dma("gpsimd", wd_n[:, :, cb * 512:(cb + 1) * 512],
                                   w_down_d[:, cb * 512:(cb + 1) * 512].rearrange("(k p) n -> p k n", p=128), s_w4b)
            else:
                tk_w4 = tk_w4n
                tk_w4p = [tk_w4n] * 3
            h2bring = Ring([h2b_all[:, i, :].rearrange("p (k n) -> p k n", k=8) for i in range(2)])
            ctring = Ring([ct_all[:, i, :].rearrange("p (s n) -> p s n", s=7) for i in range(2)])
            hTring = Ring([hT_all[:, i, :].rearrange("p (k n) -> p k n", k=11) for i in range(2)])
            if pas == 1:
                hTloring = Ring([hTlo_all[:, i, :].rearrange("p (k n) -> p k n", k=11) for i in range(3)])
            puring = Ring([0, 1, 2, 3, 6, 7])
            poring4 = Ring([4, 5])
            if pas == 1:
                x1bring = Ring([x1b_all[:, i, :] for i in range(4)])
                tmp4ring = Ring([tmp4_all[:, i, :] for i in range(1)])
                x2ring = Ring([x2_all[:, i, :] for i in range(2)])
            cw = C("convw").rearrange("p (f t) -> p f t", t=3)
            cbias = C("convb")
            ss4 = smalls[:, 64:128]
            rt4 = smalls[:, 128:192]
            rs4 = smalls[:, 192:256]
            nblk4 = N // FB

            def load4(b):
                t0 = b * FB
                hb, hd, hs = h2bring.get()
                tk1 = P.dma("sync", hb, hx2T_d[:, :, t0:t0 + FB + 2].rearrange("k p n -> p k n"), None, deps=hd)
                extra = None
                if pas == 1:
                    hlo, htd, hts = hTloring.get()
                    tk2 = P.dma("sync", hlo, hT_d[:, :, t0:t0 + FB].rearrange("k p n -> p k n"), None, deps=htd)
                    extra = (hlo, hts, tk2)
                return (hb, hs, tk1), extra

            def proc4(b, L):
                (hb, hs, tk1), extra = L
                t0 = b * FB
                hT, hdeps, hts = hTring.get()
                hoff = 0
                if pas == 1:
                    hlo, hlos, tk_hlo = extra
                tk_last_mm = None
                tk_h = None
                for fp in range(11):
                    fglob = pas * 11 + fp
                    ct, cd, cs_ = ctring.get()
                    conv_out = []
                    for part in range(2):
                        fch = part * 22 + fglob
                        pb, pd, psl = puring.get()
                        for k in range(8):
                            if pas == 0 and fp < 4:
                                wsl = wu_p0[:, k, part * 512 + fp * 128:part * 512 + (fp + 1) * 128]
                            else:
                                wsl = wu[:, k, part * 1408 + fp * 128:part * 1408 + (fp + 1) * 128]
                            tk = P.mm(banks[pb][:, 0:FB + 2], wsl,
                                      hb[:, k, :], start=(k == 0), stop=(k == 7), deps=[tk1, tk_w4p[fp // 4]] + pd, inc=(k == 7))
                        tk_last_mm = tk
                        pu = banks[pb]
                        t0_ = ct[:, part * 3 + 0, :]
                        t1_ = ct[:, part * 3 + 1, :]
                        t2_ = ct[:, part * 3 + 2, :]
                        ka = P.act(t0_, pu[:, 1:FB + 1], AF.Identity, scale=cw[:, fch, 1:2], bias=cbias[:, fch:fch + 1],
                                   deps=[tk, tk_cst] + cd)
                        kb = P.stt(t1_, pu[:, 0:FB], cw[:, fch, 0:1], t0_, ALU.mult, ALU.add, deps=[ka])
                        kc = P.stt(t2_, pu[:, 2:FB + 2], cw[:, fch, 2:3], t1_, ALU.mult, ALU.add, deps=[kb])
                        puring.used(psl, kc)
                        conv_out.append((t2_, kc))
                    sa = ct[:, 6, :]
                    ks = P.act(sa, conv_out[0][0], AF.Silu, deps=[conv_out[0][1]])
                    tk_h = P.tt("gpsimd", hT[:, hoff + fp, :], sa, conv_out[1][0], ALU.mult,
                                deps=[ks, conv_out[1][1]] + hdeps)
                    ctring.used(cs_, tk_h)
                h2bring.used(hs, tk_last_mm)
                if pas == 0:
                    tk_s = P.dma("sync", hT_d[:, :, t0:t0 + FB].rearrange("k p n -> p k n"), hT, None, deps=[tk_h])
                    hTring.used(hts, tk_s)
                    return

                def part2():
                    xs = []
                    for i in range(FB // 128):
                        xt, xd, xsl = x1bring.get()
                        tkx = P.dma("sync", xt, x1_d[t0 + i * 128:t0 + (i + 1) * 128, :], None, deps=xd)
                        xs.append((xt, xsl, tkx))
                    for i in range(FB // 128):
                        xt, xsl, tkx = xs[i]
                        t = t0 // 128 + i
                        tmp, tmd, tms = tmp4ring.get()
                        x2, x2d, x2s = x2ring.get()
                        for nh in range(2):
                            pb, pd, psl = poring4.get()
                            for fp in range(22):
                                src = hlo[:, fp, i * 128:(i + 1) * 128] if fp < 11 else hT[:, fp - 11, i * 128:(i + 1) * 128]
                                tk = P.mm(banks[pb][:, :], src, wd[:, fp, nh * 512:(nh + 1) * 512],
                                          start=(fp == 0), stop=(fp == 21), deps=[tk_h, tk_hlo, tk_w4] + pd, inc=(fp == 21))
                            tk_a = P.tt("vector", tmp[:, nh * 512:(nh + 1) * 512], banks[pb][:, :],
                                        gbc[:, 1024 + nh * 512:1024 + (nh + 1) * 512], ALU.mult, deps=[tk] + tmd)
                            poring4.used(psl, tk_a)
                            tk_b = P.tt("gpsimd", x2[:, nh * 512:(nh + 1) * 512], tmp[:, nh * 512:(nh + 1) * 512],
                                        xt[:, nh * 512:(nh + 1) * 512], ALU.add, deps=[tk_a, tkx] + x2d)
                        hTring.used(hts, tk)
                        hTloring.used(hlos, tk)
                        tmp4ring.used(tms, tk_b)
                        x1bring.used(xsl, tk_b)
                        tk_ss = P.act(junk4[:, :], x2, AF.Square, accum_out=ss4[:, t:t + 1], deps=[tk_b])
                        tk_rt = P.ts("gpsimd", rt4[:, t:t + 1], ss4[:, t:t + 1], 1.0 / D, EPS, ALU.mult, ALU.add, deps=[tk_ss])
                        tk_r = P.tt("gpsimd", rs4[:, t:t + 1], rt4[:, t:t + 1], mhalf[:, 0:1], ALU.pow, deps=[tk_rt, tk_mh])
                        tk_y = P.stt(x2, x2, rs4[:, t:t + 1], fng[:, :], ALU.mult, ALU.mult, deps=[tk_r, tk_fng])
                        tk_o = P.dma("sync", y_d[t * 128:(t + 1) * 128, :], x2, None, deps=[tk_y])
                        x2ring.used(x2s, tk_o)
                        out_toks.append(tk_o)
                return part2

            out_toks = []
            if pas == 0:
                L = load4(0)
                for b in range(nblk4):
                    Ln = load4(b + 1) if b + 1 < nblk4 else None
                    proc4(b, L)
                    L = Ln
            else:
                L = load4(0)
                pend = None
                for b in range(nblk4):
                    Ln = load4(b + 1) if b + 1 < nblk4 else None
                    nxt = proc4(b, L)
                    if pend is not None:
                        pend()
                    pend = nxt
                    L = Ln
                pend()
            P.barrier()
        if pas == 0:
            es_wp0.close()
    es_w4b.close()
    P.emit()
    return nc


def finish(nc, P, y_d):
    s = P.newsem("fin")
    z = nc.alloc_sbuf_tensor("zfin", [128, 1024], F32)
    tk = P.memset("vector", z[:, :], 0.0)
    tk2 = P.dma("sync", y_d[0:128, :], z[:, :], s, deps=[tk])
    P.wait("sync", [tk2])
    P.emit()
    return nc


def make_in_maps(inputs):
    cst, cst2, ropec, ropes = _host_consts()
    f = lambda a: np.ascontiguousarray(np.asarray(a, np.float32))
    x = f(inputs["x"]); c = f(inputs["c"]); ctx = f(inputs["ctx"]); c_ctx = f(inputs["c_ctx"])
    b_ada = f(inputs["b_ada"])[0]

    def col(v, k):
        return v.reshape(k, 128).T

    base = cst.copy()

    def put(arr, name):
        o, n = COFF[name]
        base[:, o:o + n] = np.asarray(arr, np.float32).reshape(128, n)
    put(col(b_ada, 48), "bada")
    put(col(f(inputs["norm1_g"])[0], 8), "n1g")
    put(col(f(inputs["norm2_g"])[0], 8), "n2g")
    cw = f(inputs["conv_w"])[0]
    put(np.stack([col(cw[i], 44) for i in range(3)], axis=2), "convw")
    put(col(f(inputs["conv_b"])[0], 44), "convb")
    put(np.broadcast_to(f(inputs["ret_log_decay_fwd"])[0][None, :], (128, 4)), "lgf")
    put(np.broadcast_to(f(inputs["ret_log_decay_bwd"])[0][None, :], (128, 4)), "lgb")
    rpbg = _rpb_gather(f(inputs["na_rpb"])[0]).reshape(128, -1)
    badag = np.ascontiguousarray(np.broadcast_to(np.concatenate([b_ada[2048:3072], b_ada[5120:6144]])[None, :], (128, 2048)))
    fng = np.ascontiguousarray(np.broadcast_to(f(inputs["final_norm_g"])[None, :], (128, 1024)))
    shared = {
        "ropec": ropec, "ropes": ropes, "consts2": cst2, "rpbg": np.ascontiguousarray(rpbg), "badag": badag, "fng": fng,
        "w_ada": f(inputs["w_ada"])[0], "w_in": f(inputs["w_in"])[0], "w_proj_na": f(inputs["w_proj_na"])[0],
        "w_proj_ret": f(inputs["w_proj_ret"])[0], "w_out": f(inputs["w_out"])[0], "w_up": f(inputs["w_up"])[0],
        "w_down": f(inputs["w_down"])[0],
    }
    maps = []
    for b in range(x.shape[0]):
        cb = base.copy()
        o, n = COFF["cc"]
        cb[:, o:o + n] = np.stack([col(c[b], 8), col(c_ctx, 8)], axis=2).reshape(128, 16)
        m = dict(shared)
        m["x"] = x[b]
        m["ctx"] = ctx[b]
        m["consts"] = cb
        maps.append(m)
    return maps


def kernel(**inputs):
    maps = make_in_maps(inputs)
    nc = build()
    res = run_bass_kernel_spmd(nc, maps, core_ids=list(range(len(maps))))
    return np.stack([np.asarray(r["y"], np.float32) for r in res.results], axis=0)
```
